# Optimizing a Trainium2 kernel written in Bass

```python
import math
import jax
import jax.numpy as jnp
from jax import lax
import numpy as np

D_MODEL = 4096
BATCH = 4
SEQ = 4096
DEPTH = 4
DEC_BATCH = 8
DEC_SEQ = 2048
PAST_LEN = 128

HEAD_DIM = 64
GRID_W = 64
Q_BLOCK = 128
NA_HEADS = 8
NA_ROWS = 8
NA_COLS = 16
NA_QCOLS = 16
NA_KCOLS = 32
DIFF_HEADS = 8
DIFF_D = HEAD_DIM // 2
GQA_HEADS = 8
GQA_KV_HEADS = 2
ROPE_AXIS_DIM = HEAD_DIM // 2
ROPE_THETA = 10000.0
DIL_SLOTS = 8
DIL_GROUPS = ((128, 1), (512, 4), (2048, 16))
N_DIL = 3
T5_BUCKETS = 32
T5_MAX_DIST = 128
T5_HEADS = DIFF_HEADS + N_DIL * DIL_SLOTS
N_BRANCH = 4
BRANCH_W = 8 * HEAD_DIM
GATE_RANK = 256
A_W = NA_HEADS * HEAD_DIM
B_W = DIFF_HEADS * HEAD_DIM
C_Q_W = GQA_HEADS * HEAD_DIM
C_KV_W = GQA_KV_HEADS * HEAD_DIM
D_W = N_DIL * DIL_SLOTS * HEAD_DIM
IN_SIZES = (A_W, A_W, A_W, B_W, B_W, B_W, C_Q_W, C_KV_W, C_KV_W, D_W, D_W, D_W)
D_IN = 3 * A_W + 3 * B_W + C_Q_W + 2 * C_KV_W + 3 * D_W
D_FF = ((8 * D_MODEL + 3 * 256 - 1) // (3 * 256)) * 256
LN_EPS = 1e-5
RMS_EPS = 1e-6
NEG_INF = -1e30
DEEPNORM_ALPHA = (2 * DEPTH) ** 0.25
DEEPNORM_BETA = (8 * DEPTH) ** -0.25

kernel_name = 'hybrid_gated_encoder'


def _layernorm(x, g, b):
    xf = x.astype(jnp.float32)
    mu = jnp.mean(xf, axis=-1, keepdims=True)
    var = jnp.mean(jnp.square(xf - mu), axis=-1, keepdims=True)
    return ((xf - mu) * lax.rsqrt(var + LN_EPS) * g + b).astype(x.dtype)


def _rmsnorm(x, g):
    xf = x.astype(jnp.float32)
    return (xf * lax.rsqrt(jnp.mean(xf * xf, axis=-1, keepdims=True) + RMS_EPS) * g).astype(x.dtype)


def _heads(x, n):
    return x.reshape(x.shape[0], x.shape[1], n, HEAD_DIM)


def _t5_bucket(rel):
    half = T5_BUCKETS // 2
    max_exact = half // 2
    n = jnp.abs(rel)
    nf = jnp.maximum(n, 1).astype(jnp.float32)
    large = max_exact + (jnp.log(nf / max_exact) / math.log(T5_MAX_DIST / max_exact)
                         * (half - max_exact)).astype(jnp.int32)
    large = jnp.minimum(large, half - 1)
    return jnp.where(rel > 0, half, 0) + jnp.where(n < max_exact, n, large)


def _axial_rope_tables(T):
    t = jnp.arange(T, dtype=jnp.int32)
    row = (t // GRID_W).astype(jnp.float32)
    col = (t % GRID_W).astype(jnp.float32)
    freqs = ROPE_THETA ** (-jnp.arange(0, ROPE_AXIS_DIM, 2, dtype=jnp.float32) / ROPE_AXIS_DIM)
    ang = jnp.concatenate([row[:, None] * freqs, col[:, None] * freqs], axis=-1)
    return jnp.cos(ang), jnp.sin(ang)


def _apply_rope(x, cos, sin):
    xf = x.astype(jnp.float32).reshape(*x.shape[:-1], HEAD_DIM // 2, 2)
    x1, x2 = xf[..., 0], xf[..., 1]
    c, s = cos[None, :, None, :], sin[None, :, None, :]
    out = jnp.stack([x1 * c - x2 * s, x1 * s + x2 * c], axis=-1)
    return out.reshape(x.shape).astype(x.dtype)


def _neighbourhood_attention(q, k, v, rpb):
    B, T, H, dh = q.shape
    R = T // GRID_W
    wr = min(NA_ROWS, R)
    n_cb = GRID_W // NA_QCOLS
    qg = q.reshape(B, R, GRID_W, H, dh)
    kg = k.reshape(B, R, GRID_W, H, dh)
    vg = v.reshape(B, R, GRID_W, H, dh)
    qcol = jnp.arange(GRID_W, dtype=jnp.int32).reshape(n_cb, NA_QCOLS)
    c0 = jnp.clip(qcol - NA_COLS // 2, 0, GRID_W - NA_COLS)
    kstart = jnp.clip(jnp.arange(n_cb, dtype=jnp.int32) * NA_QCOLS - NA_COLS // 2, 0, GRID_W - NA_KCOLS)
    kcol = kstart[:, None] + jnp.arange(NA_KCOLS, dtype=jnp.int32)
    col_ok = (kcol[:, None, :] >= c0[..., None]) & (kcol[:, None, :] < c0[..., None] + NA_COLS)
    dcol = jnp.clip(kcol[:, None, :] - qcol[..., None] + NA_COLS - 1, 0, 2 * NA_COLS - 2)
    bias_col = rpb[:, :, dcol]
    scale = dh ** -0.5

    def row_step(r):
        r0 = jnp.clip(r - wr // 2, 0, R - wr)
        kr = lax.dynamic_slice_in_dim(kg, r0, wr, axis=1)[:, :, kcol]
        vr = lax.dynamic_slice_in_dim(vg, r0, wr, axis=1)[:, :, kcol]
        qr = lax.dynamic_index_in_dim(qg, r, axis=1, keepdims=False).reshape(B, n_cb, NA_QCOLS, H, dh)
        s = jnp.einsum('bcqhd,bicjhd->bhcqij', qr, kr, preferred_element_type=jnp.float32) * scale
        drow = r0 + jnp.arange(wr, dtype=jnp.int32) - r + NA_ROWS - 1
        bias = bias_col[:, drow].transpose(0, 2, 3, 1, 4)
        s = jnp.where(col_ok[:, :, None, :], s + bias, NEG_INF)
        p = jax.nn.softmax(s, axis=(-2, -1))
        o = jnp.einsum('bhcqij,bicjhd->bcqhd', p.astype(v.dtype), vr)
        return o.reshape(B, GRID_W, H, dh)

    o = lax.map(row_step, jnp.arange(R, dtype=jnp.int32))
    return o.transpose(1, 0, 2, 3, 4).reshape(B, T, H * dh)


def _diff_attention(q, k, v, t5_tab, lam, lam_init, subln_g):
    B, T, H, _ = q.shape
    nb = T // Q_BLOCK
    qb = (q * DIFF_D ** -0.5).reshape(B, nb, Q_BLOCK, H, 2, DIFF_D).transpose(1, 0, 2, 3, 4, 5)
    k2 = k.reshape(B, T, H, 2, DIFF_D)
    kpos = jnp.arange(T, dtype=jnp.int32)

    def block(args):
        qi, i = args
        qpos = i * Q_BLOCK + jnp.arange(Q_BLOCK, dtype=jnp.int32)
        bias = t5_tab[_t5_bucket(kpos[None, :] - qpos[:, None])].transpose(2, 0, 1)
        s = jnp.einsum('bqhcd,bkhcd->bhcqk', qi, k2, preferred_element_type=jnp.float32) + bias[None, :, None]
        m = jnp.max(s, axis=-1, keepdims=True)
        p = jnp.exp(s - m)
        l = jnp.sum(p, axis=-1)
        o = jnp.einsum('bhcqk,bkhd->bqhcd', p.astype(v.dtype), v, preferred_element_type=jnp.float32)
        o = o / l.transpose(0, 3, 1, 2)[..., None]
        return (o[..., 0, :] - lam * o[..., 1, :]).astype(v.dtype)

    o = lax.map(block, (qb, jnp.arange(nb, dtype=jnp.int32)))
    o = o.transpose(1, 0, 2, 3, 4).reshape(B, T, H, 2 * DIFF_D)
    o = _rmsnorm(o, subln_g) * (1.0 - lam_init)
    return o.reshape(B, T, H * 2 * DIFF_D)


def _gqa_attention(q, k, v):
    B, T, H, dh = q.shape
    kvh = k.shape[2]
    nb = T // Q_BLOCK
    qb = (q * dh ** -0.5).reshape(B, nb, Q_BLOCK, kvh, H // kvh, dh).transpose(1, 0, 2, 3, 4, 5)

    def block(qi):
        s = jnp.einsum('bqngd,bknd->bngqk', qi, k, preferred_element_type=jnp.float32)
        m = jnp.max(s, axis=-1, keepdims=True)
        p = jnp.exp(s - m)
        l = jnp.sum(p, axis=-1)
        o = jnp.einsum('bngqk,bknd->bqngd', p.astype(v.dtype), v, preferred_element_type=jnp.float32)
        return (o / l.transpose(0, 3, 1, 2)[..., None]).astype(v.dtype)

    o = lax.map(block, qb)
    return o.transpose(1, 0, 2, 3, 4, 5).reshape(B, T, H * dh)


def _dilated_group(q, k, v, dil, half, bias_tab):
    B, T, S, dh = q.shape
    L = T // dil
    nqb = -(-L // half)
    Lp = nqb * half

    def sub(x):
        return x.reshape(B, L, dil, S, dh).transpose(0, 2, 1, 3, 4).reshape(B * dil, L, S, dh)

    def windows(x):
        xp = jnp.pad(sub(x), ((0, 0), (half, Lp - L + half), (0, 0), (0, 0)))
        xp = xp.reshape(B * dil, nqb + 2, half, S, dh)
        return jnp.concatenate([xp[:, :-2], xp[:, 1:-1], xp[:, 2:]], axis=2)

    qs = jnp.pad(sub(q), ((0, 0), (0, Lp - L), (0, 0), (0, 0))).reshape(B * dil, nqb, half, S, dh)
    kw, vw = windows(k), windows(v)
    qpos = jnp.arange(Lp, dtype=jnp.int32).reshape(nqb, half)
    kpos = jnp.arange(nqb, dtype=jnp.int32)[:, None] * half - half + jnp.arange(3 * half, dtype=jnp.int32)
    rel = kpos[:, None, :] - qpos[:, :, None]
    valid = (jnp.abs(rel) <= half) & (kpos[:, None, :] >= 0) & (kpos[:, None, :] < L)
    bias = bias_tab[_t5_bucket(rel * dil)].transpose(3, 0, 1, 2)
    s = jnp.einsum('xnqsd,xnksd->xsnqk', qs, kw, preferred_element_type=jnp.float32) * dh ** -0.5
    s = jnp.where(valid, s + bias, NEG_INF)
    m = jnp.max(s, axis=-1, keepdims=True)
    p = jnp.exp(s - m)
    den = jnp.sum(p, axis=-1, keepdims=True)
    o = jnp.einsum('xsnqk,xnksd->xnqsd', (p / den).astype(v.dtype), vw)
    lse = (m + jnp.log(den))[..., 0]
    o = o.reshape(B, dil, Lp, S, dh)[:, :, :L].transpose(0, 2, 1, 3, 4).reshape(B, T, S, dh)
    lse = lse.reshape(B, dil, S, Lp)[..., :L].transpose(0, 3, 1, 2).reshape(B, T, S)
    return o, lse


def _encoder_trunk(x, ln_in_g, ln_in_b, w_in, na_rpb, qk_norm_g, diff_lambda, diff_subln_g,
                   t5_table, w_gate_down, w_gate_up, b_gate, w_branch, w_out, ln1_g, ln1_b,
                   w_ffn_in, w_ffn_out, ln2_g, ln2_b):
    B, T, _ = x.shape
    cos, sin = _axial_rope_tables(T)
    splits = [int(c) for c in np.cumsum(IN_SIZES)[:-1]]
    x = _layernorm(x, ln_in_g, ln_in_b)
    for l in range(DEPTH):
        lam_init = 0.8 - 0.6 * math.exp(-0.3 * l)
        h = jnp.einsum('btd,de->bte', x, w_in[l])
        qa, ka, va, qb, kb, vb, qc, kc, vc, qd, kd, vd = jnp.split(h, splits, axis=-1)
        o_a = _neighbourhood_attention(_heads(qa, NA_HEADS), _heads(ka, NA_HEADS),
                                       _heads(va, NA_HEADS), na_rpb[l])
        lq = diff_lambda[l].astype(jnp.float32)
        lam = jnp.exp(jnp.sum(lq[0] * lq[1])) - jnp.exp(jnp.sum(lq[2] * lq[3])) + lam_init
        o_b = _diff_attention(_heads(qb, DIFF_HEADS), _heads(kb, DIFF_HEADS), _heads(vb, DIFF_HEADS),
                              t5_table[:, :DIFF_HEADS], lam, lam_init, diff_subln_g[l])
        qc_h = _apply_rope(_rmsnorm(_heads(qc, GQA_HEADS), qk_norm_g[l, 0]), cos, sin)
        kc_h = _apply_rope(_rmsnorm(_heads(kc, GQA_KV_HEADS), qk_norm_g[l, 1]), cos, sin)
        o_c = _gqa_attention(qc_h, kc_h, _heads(vc, GQA_KV_HEADS))
        qd_g = qd.reshape(B, T, N_DIL, DIL_SLOTS, HEAD_DIM)
        kd_g = kd.reshape(B, T, N_DIL, DIL_SLOTS, HEAD_DIM)
        vd_g = vd.reshape(B, T, N_DIL, DIL_SLOTS, HEAD_DIM)
        outs, lses = [], []
        for g, (window, dil) in enumerate(DIL_GROUPS):
            lo = DIFF_HEADS + g * DIL_SLOTS
            o_g, lse_g = _dilated_group(qd_g[:, :, g], kd_g[:, :, g], vd_g[:, :, g], dil,
                                        window // (2 * dil), t5_table[:, lo:lo + DIL_SLOTS])
            outs.append(o_g)
            lses.append(lse_g)
        wts = jax.nn.softmax(jnp.stack(lses, axis=-1), axis=-1)
        o_d = jnp.einsum('btsgd,btsg->btsd', jnp.stack(outs, axis=3), wts.astype(x.dtype))
        o_d = o_d.reshape(B, T, DIL_SLOTS * HEAD_DIM)
        gz = jnp.einsum('btd,dr->btr', x, w_gate_down[l])
        gates = jax.nn.sigmoid(jnp.einsum('btr,re->bte', gz, w_gate_up[l]).reshape(B, T, N_BRANCH, D_MODEL)
                               + b_gate[l])
        merged = None
        for n, o in enumerate((o_a, o_b, o_c, o_d)):
            term = gates[:, :, n] * jnp.einsum('btc,cd->btd', o, w_branch[l, n])
            merged = term if merged is None else merged + term
        mix = jnp.einsum('btd,de->bte', merged, w_out[l])
        x = _layernorm(DEEPNORM_ALPHA * x + mix, ln1_g[l], ln1_b[l])
        u = jnp.einsum('btd,df->btf', x, w_ffn_in[l])
        ff = jax.nn.silu(u[..., :D_FF]) * u[..., D_FF:]
        f = jnp.einsum('btf,fd->btd', ff, w_ffn_out[l])
        x = _layernorm(DEEPNORM_ALPHA * x + f, ln2_g[l], ln2_b[l])
    return x


def setup_inputs(seed: int = 0) -> dict:
    key = jax.random.key(seed)
    ks = jax.random.split(key, 21)

    def nrm(k, shape, scale):
        return jax.random.normal(k, shape, jnp.float32) * scale

    return {
        'x_prompt': nrm(ks[0], (BATCH, SEQ, D_MODEL), 1.0),
        'x_sample': nrm(ks[1], (DEC_BATCH, DEC_SEQ, D_MODEL), 1.0),
        'ln_in_g': 1.0 + nrm(ks[2], (D_MODEL,), 0.01),
        'ln_in_b': nrm(ks[3], (D_MODEL,), 0.01),
        'w_in': nrm(ks[4], (DEPTH, D_MODEL, D_IN), D_MODEL ** -0.5),
        'na_rpb': nrm(ks[5], (DEPTH, NA_HEADS, 2 * NA_ROWS - 1, 2 * NA_COLS - 1), 0.1),
        'qk_norm_g': 1.0 + nrm(ks[6], (DEPTH, 2, HEAD_DIM), 0.01),
        'diff_lambda': nrm(ks[7], (DEPTH, 4, DIFF_D), 0.1),
        'diff_subln_g': 1.0 + nrm(ks[8], (DEPTH, 2 * DIFF_D), 0.01),
        't5_table': nrm(ks[9], (T5_BUCKETS, T5_HEADS), 0.1),
        'w_gate_down': nrm(ks[10], (DEPTH, D_MODEL, GATE_RANK), D_MODEL ** -0.5),
        'w_gate_up': nrm(ks[11], (DEPTH, GATE_RANK, N_BRANCH * D_MODEL), GATE_RANK ** -0.5),
        'b_gate': nrm(ks[12], (DEPTH, N_BRANCH, D_MODEL), 0.01),
        'w_branch': nrm(ks[13], (DEPTH, N_BRANCH, BRANCH_W, D_MODEL), BRANCH_W ** -0.5),
        'w_out': nrm(ks[14], (DEPTH, D_MODEL, D_MODEL), D_MODEL ** -0.5 * DEEPNORM_BETA),
        'ln1_g': 1.0 + nrm(ks[15], (DEPTH, D_MODEL), 0.01),
        'ln1_b': nrm(ks[16], (DEPTH, D_MODEL), 0.01),
        'w_ffn_in': nrm(ks[17], (DEPTH, D_MODEL, 2 * D_FF), D_MODEL ** -0.5),
        'w_ffn_out': nrm(ks[18], (DEPTH, D_FF, D_MODEL), D_FF ** -0.5 * DEEPNORM_BETA),
        'ln2_g': 1.0 + nrm(ks[19], (DEPTH, D_MODEL), 0.01),
        'ln2_b': nrm(ks[20], (DEPTH, D_MODEL), 0.01),
    }


def reference(x_prompt, x_sample, ln_in_g, ln_in_b, w_in, na_rpb, qk_norm_g, diff_lambda,
              diff_subln_g, t5_table, w_gate_down, w_gate_up, b_gate, w_branch, w_out, ln1_g,
              ln1_b, w_ffn_in, w_ffn_out, ln2_g, ln2_b):
    y_prompt = _encoder_trunk(x_prompt, ln_in_g, ln_in_b, w_in, na_rpb, qk_norm_g, diff_lambda,
                              diff_subln_g, t5_table, w_gate_down, w_gate_up, b_gate, w_branch,
                              w_out, ln1_g, ln1_b, w_ffn_in, w_ffn_out, ln2_g, ln2_b)
    y_sample = _encoder_trunk(x_sample, ln_in_g, ln_in_b, w_in, na_rpb, qk_norm_g, diff_lambda,
                              diff_subln_g, t5_table, w_gate_down, w_gate_up, b_gate, w_branch,
                              w_out, ln1_g, ln1_b, w_ffn_in, w_ffn_out, ln2_g, ln2_b)
    return (y_prompt, y_sample)
```

```python
import math
from contextlib import ExitStack

import numpy as np
import ml_dtypes

import concourse.bass as bass
import concourse.mybir as mybir
from concourse.bass_utils import run_bass_kernel_spmd

F32 = mybir.dt.float32
BF16 = mybir.dt.bfloat16
AF = mybir.ActivationFunctionType
ALU = mybir.AluOpType
AX = mybir.AxisListType

D = 4096
DEPTH = 4
HD = 64
D_IN = 8448
D_FF = 11008
KC = D // 128
TT = 512
NEG = -30000.0
ALPHA = (2 * DEPTH) ** 0.25
LN_EPS = 1e-5
RMS_EPS = 1e-6
DIL = (1, 4, 16)
FV = 3072

R_QA, R_KA, R_QB, R_KB, R_QD, R_KD, R_QC, R_KC = 0, 512, 1024, 1536, 2048, 3584, 5120, 5632
QK_ROWS = 5760
C_VA, C_VB, C_VC, C_VD = 0, 512, 1024, 1152
V_COLS = 2688


class Buf:
    __slots__ = ("w", "r", "dsem", "name")

    def __init__(self, name=""):
        self.w = {}
        self.r = {}
        self.dsem = None
        self.name = name


class Sched:
    def __init__(self, nc, n_dsem=96):
        self.nc = nc
        self.eng = {"pe": nc.tensor, "act": nc.scalar, "dve": nc.vector, "pool": nc.gpsimd, "sp": nc.sync}
        self.semh = []
        self.psem = {}
        self.pcnt = {}
        for k in ("pe", "act", "dve", "pool"):
            self.psem[k] = self._new_sem("prog_" + k)
            self.pcnt[k] = 0
        self.free_dsems = [self._new_sem("dma%d" % i) for i in range(n_dsem)]
        self.dcount = {s: 0 for s in self.free_dsems}
        self.waited = {k: {} for k in self.eng}
        self.latest = {}
        self.phase_dsems = []
        self.pe_pending = False
        self.swq = []

    def _new_sem(self, name):
        self.semh.append(self.nc.alloc_semaphore(name=name))
        return len(self.semh) - 1

    def _deps(self, e, reads, writes, pwrites):
        need = {}
        for b in reads:
            for s, v in b.w.items():
                if need.get(s, 0) < v:
                    need[s] = v
        for bl in (writes, pwrites):
            for b in bl:
                for d in (b.w, b.r):
                    for s, v in d.items():
                        if need.get(s, 0) < v:
                            need[s] = v
        own = self.psem["pe"] if e == "pe" else None
        wd = self.waited[e]
        en = self.eng[e]
        for s, v in need.items():
            if s == own:
                continue
            if wd.get(s, 0) < v:
                en.wait_ge(self.semh[s], v)
                wd[s] = v

    def _mark(self, s, val, reads, writes, pwrites):
        for b in reads:
            if b.r.get(s, 0) < val:
                b.r[s] = val
        for b in writes:
            b.w = {s: val}
            b.r = {}
        for b in pwrites:
            if b.w.get(s, 0) < val:
                b.w[s] = val
        if self.latest.get(s, 0) < val:
            self.latest[s] = val

    def op(self, e, fn, reads=(), writes=(), pwrites=(), signal=True):
        self._deps(e, reads, writes, pwrites)
        ins = fn(self.eng[e])
        s = self.psem[e]
        if signal:
            self.pcnt[e] += 1
            ins.then_inc(self.semh[s], 1)
            val = self.pcnt[e]
            if e == "pe":
                self.pe_pending = False
        else:
            assert e == "pe"
            val = self.pcnt[e] + 1
            self.pe_pending = True
        self._mark(s, val, reads, writes, pwrites)
        return ins

    def _sw_throttle(self, q, out, s):
        if q != "pool":
            return
        nd = 2
        for d in out.shape[:-1]:
            nd *= d
        nd = nd // 16 + 4
        wd = self.waited["pool"]
        while self.swq and sum(x[2] for x in self.swq) + nd > 900:
            s0, v0, _ = self.swq.pop(0)
            if wd.get(s0, 0) < v0:
                self.eng["pool"].wait_ge(self.semh[s0], v0)
                wd[s0] = v0
        self.swq.append((s, self.dcount[s] + 16, nd))

    def dma(self, q, out, in_, home, reads=(), writes=(), pwrites=()):
        self._deps(q, reads, writes, pwrites)
        if home.dsem is None:
            home.dsem = self.free_dsems.pop()
            self.phase_dsems.append(home.dsem)
        s = home.dsem
        self._sw_throttle(q, out, s)
        ins = self.eng[q].dma_start(out=out, in_=in_)
        self.dcount[s] += 16
        ins.then_inc(self.semh[s], 16)
        self._mark(s, self.dcount[s], reads, writes, pwrites)
        return ins

    def dma_more(self, q, out, in_, home, buf):
        s = home.dsem
        self._sw_throttle(q, out, s)
        ins = self.eng[q].dma_start(out=out, in_=in_)
        self.dcount[s] += 16
        ins.then_inc(self.semh[s], 16)
        buf.w[s] = self.dcount[s]
        self.latest[s] = self.dcount[s]

    def barrier(self):
        assert not self.pe_pending
        for e in self.eng:
            wd = self.waited[e]
            for s, v in self.latest.items():
                if e == "pe" and s == self.psem["pe"]:
                    continue
                if wd.get(s, 0) < v:
                    self.eng[e].wait_ge(self.semh[s], v)
                    wd[s] = v
        self.free_dsems.extend(self.phase_dsems)
        self.phase_dsems = []


class Ring:
    def __init__(self, tiles):
        self.tiles = tiles
        self.bufs = [Buf() for _ in tiles]
        self.i = -1

    def next(self):
        self.i = (self.i + 1) % len(self.tiles)
        return self.tiles[self.i], self.bufs[self.i]


def _t5_bucket_np(rel):
    import jax
    import jax.numpy as jnp
    with jax.default_device(jax.devices("cpu")[0]):
        rel = jnp.asarray(rel, dtype=jnp.int32)
        half = 16
        max_exact = 8
        n = jnp.abs(rel)
        nf = jnp.maximum(n, 1).astype(jnp.float32)
        large = max_exact + (jnp.log(nf / max_exact) / math.log(128 / max_exact)
                             * (half - max_exact)).astype(jnp.int32)
        large = jnp.minimum(large, half - 1)
        out = jnp.where(rel > 0, half, 0) + jnp.where(n < max_exact, n, large)
        return np.asarray(out)


def make_consts(seg, cross):
    T = 2 * seg
    pos = np.arange(T) if cross else (np.arange(T) % seg)
    freqs = (10000.0 ** (-np.arange(0, 32, 2, dtype=np.float32) / np.float32(32))).astype(np.float32)
    row = (pos // 64).astype(np.float32)
    col = (pos % 64).astype(np.float32)
    ang = np.concatenate([row[:, None] * freqs, col[:, None] * freqs], axis=-1).astype(np.float32)
    c, s = np.cos(ang).astype(np.float32), np.sin(ang).astype(np.float32)
    cc = np.repeat(c, 2, axis=1)
    ss = np.repeat(s, 2, axis=1)
    ss[:, 0::2] *= -1.0
    rope_c = np.ascontiguousarray(cc.reshape(T // 128, 128, 64).transpose(1, 0, 2))
    rope_s = np.ascontiguousarray(ss.reshape(T // 128, 128, 64).transpose(1, 0, 2))
    nqt = T // 512
    rs = (T // 64) if cross else (seg // 64)
    rm = np.zeros((128, nqt, 8, 8), np.float32)
    for j in range(nqt):
        for slot in range(8):
            kt = 4 * j - 2 + slot
            if kt < 0 or kt >= T // 128:
                continue
            for krl in range(2):
                kr = 2 * kt + krl
                for qrl in range(8):
                    qr = 8 * j + qrl
                    if kr // rs != qr // rs:
                        continue
                    r0 = min(max(qr % rs - 4, 0), rs - 8)
                    if r0 <= kr % rs < r0 + 8:
                        rm[krl * 64:(krl + 1) * 64, j, slot, qrl] = 1.0
    cm = np.full((64, 64), NEG, np.float32)
    for qc in range(64):
        c0 = min(max(qc - 8, 0), 48)
        cm[c0:c0 + 16, qc] = 0.0
    cm = np.concatenate([cm, cm], axis=0)
    rel = np.arange(FV) - FV // 2
    bk = _t5_bucket_np(rel)
    oh = np.zeros((32, FV), np.float32)
    oh[bk, np.arange(FV)] = 1.0
    mv = np.zeros((32, FV), np.float32)
    for g, dil in enumerate(DIL):
        ok = (rel % dil == 0) & (np.abs(rel) <= 64 * dil)
        mv[8 + 8 * g: 16 + 8 * g, :] = np.where(ok, 0.0, NEG)[None, :]
    return {
        "rope_c": rope_c, "rope_s": rope_s, "rowmask": rm, "colmask": cm, "oh_t5": oh, "maskvec": mv,
        "crossb": np.full((128, 1), 0.0 if cross else NEG, np.float32),
        "ident": np.eye(128, dtype=np.float32).astype(ml_dtypes.bfloat16),
    }


def cols(v):
    v = np.asarray(v, np.float32)
    return np.ascontiguousarray(v.reshape(-1, 128).T)


def tile_major(x):
    T = x.shape[0]
    return np.ascontiguousarray(x.reshape(T // TT, TT, KC, 128).transpose(0, 3, 2, 1))


def untile_major(y):
    nt = y.shape[0]
    return np.ascontiguousarray(y.transpose(0, 3, 2, 1).reshape(nt * TT, D))


class Prog:
    def __init__(self, n_layers, seg, debug=()):
        self.L = n_layers
        self.seg = seg
        self.T = 2 * seg
        self.NT = self.T // TT
        self.KT = self.T // 128
        self.debug = set(debug)
        nc = bass.Bass("TRN2", target_bir_lowering=False)
        self.nc = nc
        self.s = Sched(nc)
        L, T, NT = self.L, self.T, self.NT

        def inp(name, shape, dt=F32):
            return nc.dram_tensor(name, list(shape), dt, kind="ExternalInput")

        def scr(name, shape, dt):
            if name in self.debug:
                return nc.dram_tensor(name, list(shape), dt, kind="ExternalOutput")
            return nc.dram_tensor(name, list(shape), dt)

        self.xin = inp("xin", [NT, 128, KC, TT])
        self.ln_in_g = inp("ln_in_g", [128, KC])
        self.ln_in_b = inp("ln_in_b", [128, KC])
        self.w_in = inp("w_in", [L, D, D_IN])
        self.na_rpb = inp("na_rpb", [L, 8, 31, 127])
        self.qk_norm_g = inp("qk_norm_g", [L, 2, 64])
        self.diff_lambda = inp("diff_lambda", [L, 128])
        self.diff_subln_g = inp("diff_subln_g", [L, 64])
        self.t5_table = inp("t5_table", [32, 32])
        self.w_gd = inp("w_gate_down", [L, D, 256])
        self.w_gu = inp("w_gate_up", [L, 256, 4 * D])
        self.b_gate = inp("b_gate", [L, 128, 4 * KC])
        self.w_br = inp("w_branch", [L, 2048, D])
        self.w_out = inp("w_out", [L, D, D])
        self.ln1_g = inp("ln1_g", [L, 128, KC])
        self.ln1_b = inp("ln1_b", [L, 128, KC])
        self.w_fi = inp("w_ffn_in", [L, D, 2 * D_FF])
        self.w_fo = inp("w_ffn_out", [L, D_FF, D])
        self.ln2_g = inp("ln2_g", [L, 128, KC])
        self.ln2_b = inp("ln2_b", [L, 128, KC])
        self.rope_c = inp("rope_c", [128, T // 128, 64])
        self.rope_s = inp("rope_s", [128, T // 128, 64])
        self.rowmask = inp("rowmask", [128, NT, 8, 8])
        self.colmask = inp("colmask", [128, 64])
        self.oh_t5 = inp("oh_t5", [32, FV])
        self.maskvec = inp("maskvec", [32, FV])
        self.crossb_in = inp("crossb", [128, 1])
        self.ident_in = inp("ident", [128, 128], BF16)
        self.yout = nc.dram_tensor("yout", [NT, 128, KC, TT], F32, kind="ExternalOutput")
        self.xres = scr("xres", [NT, 128, KC, TT], F32)
        self.xbf = scr("xbf", [NT, 128, KC, TT], BF16)
        self.ypre = scr("ypre", [NT, 128, KC, TT], F32)
        self.qkT = scr("qkT", [QK_ROWS, T], BF16)
        self.vtok = scr("vtok", [T, V_COLS], BF16)
        self.oT = scr("oT", [2048, T], BF16)
        self.mrg = scr("mrg", [NT, 128, KC, TT], BF16)
        self.ffT = scr("ffT", [NT, 128, D_FF // 128, TT], BF16)
        self.fvec = scr("fvec", [32, FV], F32)

    def sb(self, st, name, shape, dt):
        self._uid = getattr(self, "_uid", 0) + 1
        return st.enter_context(self.nc.sbuf_tensor("sb%d_%s" % (self._uid, name), list(shape), dt))

    def build(self):
        nc, s = self.nc, self.s
        with ExitStack() as g:
            self.ps = [g.enter_context(nc.psum_tensor("ps%d" % i, [128, 512], F32)) for i in range(7)]
            self.psb = Ring(self.ps)
            self.pst = g.enter_context(nc.psum_tensor("pst", [128, 1024], BF16))
            self.pstb = Buf("pst")
            self.ident = self.sb(g, "ident", [128, 128], BF16)
            self.ones_f = self.sb(g, "ones_f", [128, 128], F32)
            self.crossb = self.sb(g, "crossb", [128, 1], F32)
            self.zero_c = self.sb(g, "zero_c", [128, 1], F32)
            self.cb = Buf("consts")
            s.dma("sp", self.ident[:], self.ident_in[:, :], self.cb, pwrites=[self.cb])
            s.dma("sp", self.crossb[:], self.crossb_in[:, :], self.cb, pwrites=[self.cb])
            s.op("dve", lambda e: e.memset(self.ones_f[:], 1.0), pwrites=[self.cb])
            s.op("dve", lambda e: e.memset(self.zero_c[:], 0.0), pwrites=[self.cb])
            s.barrier()
            self.phase_ln(self.xin, None, self.ln_in_g, self.ln_in_b, None, last=False)
            self.phase_setup(g)
            for l in range(self.L):
                if "stop_p1" in self.debug:
                    self.phase_p1(l)
                    break
                if "only_attn" in self.debug:
                    self.phase_p1(l)
                    for ph in self.debug_phases:
                        getattr(self, ph)(l)
                    break
                if self.layer(l):
                    break
            s.barrier()
        return nc

    def phase_ln(self, src, l, gsrc, bsrc, dst_unused, last):
        nc, s = self.nc, self.s
        with ExitStack() as st:
            gcol = self.sb(st, "ln_g", [128, KC], F32)
            bcol = self.sb(st, "ln_b", [128, KC], F32)
            cbuf = Buf()
            if l is None:
                s.dma("sp", gcol[:], gsrc[:, :], cbuf, pwrites=[cbuf])
                s.dma("sp", bcol[:], bsrc[:, :], cbuf, pwrites=[cbuf])
            else:
                s.dma("sp", gcol[:], gsrc[l], cbuf, pwrites=[cbuf])
                s.dma("sp", bcol[:], bsrc[l], cbuf, pwrites=[cbuf])
            HT = 256
            yr = Ring([self.sb(st, "ln_y%d" % i, [128, KC, HT], F32) for i in range(2)])
            yqs = [[Buf() for _ in range(4)] for _ in range(2)]
            br = Ring([self.sb(st, "ln_o%d" % i, [128, KC, HT], BF16) for i in range(2)])
            sq = self.sb(st, "ln_sq", [128, 8, HT], F32)
            sqb = Buf()
            acc = self.sb(st, "ln_acc", [128, 5, HT], F32)
            accb = Buf()
            mean = self.sb(st, "ln_mean", [128, HT], F32)
            rstd = self.sb(st, "ln_rstd", [128, HT], F32)
            tmp = self.sb(st, "ln_tmp", [128, HT], F32)
            stb = Buf()
            def lnload(it_):
                yt_, _ = yr.next()
                yq_ = yqs[yr.i]
                s.dma("sp", yt_[:], src[it_ // 2, :, :, (it_ % 2) * HT:(it_ % 2 + 1) * HT], yq_[0], writes=yq_)
                return yt_, yq_

            nxt_ln = lnload(0)
            for it in range(self.NT * 2):
                tt, hh = it // 2, it % 2
                yt, yq = nxt_ln
                if it + 1 < self.NT * 2:
                    nxt_ln = lnload(it + 1)
                ot, ob = br.next()
                s.op("dve", lambda e: e.tensor_reduce(out=acc[:, 0, :], in_=yt[:].rearrange("p k j -> p j k"),
                                                      axis=AX.X, op=ALU.add), reads=yq, pwrites=[accb])
                for q in range(4):
                    s.op("act", lambda e: e.activation(out=sq[:], in_=yt[:, 8 * q:8 * q + 8, :], func=AF.Square),
                         reads=[yq[q]], writes=[sqb])
                    s.op("dve", lambda e: e.tensor_reduce(out=acc[:, 1 + q, :], in_=sq[:].rearrange("p k j -> p j k"),
                                                          axis=AX.X, op=ALU.add), reads=[sqb], pwrites=[accb])
                p1, p1b = self.psb.next()
                s.op("pe", lambda e: e.matmul(p1[:, 0:HT], lhsT=self.ones_f[:], rhs=acc[:, 0, :], start=True, stop=True),
                     reads=[accb, self.cb], writes=[p1b])
                p2, p2b = self.psb.next()
                for q in range(4):
                    s.op("pe", lambda e: e.matmul(p2[:, 0:HT], lhsT=self.ones_f[:], rhs=acc[:, 1 + q, :],
                                                  start=(q == 0), stop=(q == 3)),
                         reads=[accb, self.cb], writes=[p2b] if q == 0 else (), pwrites=() if q == 0 else [p2b],
                         signal=(q == 3))
                s.op("dve", lambda e: e.tensor_scalar(out=mean[:], in0=p1[:, 0:HT], scalar1=1.0 / D, scalar2=None,
                                                      op0=ALU.mult), reads=[p1b], writes=[stb])
                s.op("dve", lambda e: e.tensor_tensor(out=tmp[:], in0=mean[:], in1=mean[:], op=ALU.mult),
                     reads=[stb], pwrites=[stb])
                s.op("dve", lambda e: e.scalar_tensor_tensor(out=tmp[:], in0=p2[:, 0:HT], scalar=1.0 / D, in1=tmp[:],
                                                             op0=ALU.mult, op1=ALU.subtract),
                     reads=[p2b, stb], pwrites=[stb])
                s.op("dve", lambda e: e.tensor_scalar(out=tmp[:], in0=tmp[:], scalar1=LN_EPS, scalar2=None, op0=ALU.add),
                     reads=[stb], pwrites=[stb])
                s.op("act", lambda e: e.activation(out=tmp[:], in_=tmp[:], func=AF.Sqrt), reads=[stb], pwrites=[stb])
                s.op("dve", lambda e: e.reciprocal(out=rstd[:], in_=tmp[:]), reads=[stb], pwrites=[stb])
                mean_b = mean[:].unsqueeze(1).to_broadcast([128, 8, HT])
                rstd_b = rstd[:].unsqueeze(1).to_broadcast([128, 8, HT])
                for q in range(4):
                    sl = slice(8 * q, 8 * q + 8)
                    s.op("dve", lambda e: e.tensor_tensor(out=yt[:, sl, :], in0=yt[:, sl, :], in1=mean_b, op=ALU.subtract),
                         reads=[stb], pwrites=[yq[q]])
                    s.op("pool", lambda e: e.tensor_tensor(out=yt[:, sl, :], in0=yt[:, sl, :], in1=rstd_b, op=ALU.mult),
                         reads=[stb], pwrites=[yq[q]])
                    for k in range(8 * q, 8 * q + 8):
                        s.op("act", lambda e: e.activation(out=yt[:, k, :], in_=yt[:, k, :], func=AF.Identity,
                                                           scale=gcol[:, k:k + 1], bias=bcol[:, k:k + 1]),
                             reads=[cbuf], pwrites=[yq[q]])
                    if not last:
                        s.op("pool", lambda e: e.tensor_copy(out=ot[:, sl, :], in_=yt[:, sl, :]), reads=[yq[q]],
                             writes=[ob] if q == 0 else (), pwrites=() if q == 0 else [ob])
                if last:
                    s.dma("pool", self.yout[tt, :, :, hh * HT:(hh + 1) * HT], yt[:], yq[0], reads=yq)
                else:
                    s.dma("pool", self.xres[tt, :, :, hh * HT:(hh + 1) * HT], yt[:], yq[0], reads=yq)
                    s.dma("pool", self.xbf[tt, :, :, hh * HT:(hh + 1) * HT], ot[:], ob, reads=[ob])
            s.barrier()

    def phase_p1(self, l):
        nc, s = self.nc, self.s
        T, NT = self.T, self.NT
        isq = 0.125
        blocks = [
            (0, 512, "fm", R_QA, isq), (512, 512, "fm", R_KA, 1.0),
            (1536, 512, "fm", R_QB, 32 ** -0.5), (2048, 512, "fm", R_KB, 1.0),
            (3840, 512, "fm", R_QD, isq), (4352, 512, "fm", R_QD + 512, isq), (4864, 512, "fm", R_QD + 1024, isq),
            (5376, 512, "fm", R_KD, 1.0), (5888, 512, "fm", R_KD + 512, 1.0), (6400, 512, "fm", R_KD + 1024, 1.0),
            (1024, 512, "tm", C_VA, 1.0), (2560, 512, "tm", C_VB, 1.0),
            (3072, 512, "qc", 0, 1.0), (3584, 256, "kcvc", 0, 1.0),
            (6912, 512, "tm", C_VD, 1.0), (7424, 512, "tm", C_VD + 512, 1.0), (7936, 512, "tm", C_VD + 1024, 1.0),
        ]
        with ExitStack() as st:
            wr = Ring([self.sb(st, "p1_w%d" % i, [128, KC, 512], BF16) for i in range(2)])
            xr = Ring([self.sb(st, "p1_x%d" % i, [128, KC, TT], BF16) for i in range(2)])
            sr = Ring([self.sb(st, "p1_s%d" % i, [128, 512], BF16) for i in range(4)])
            gq = self.sb(st, "p1_gq", [128, 512], F32)
            gk = self.sb(st, "p1_gk", [128, 128], F32)
            rc = self.sb(st, "p1_rc", [128, T // 128, 64], F32)
            rs_ = self.sb(st, "p1_rs", [128, T // 128, 64], F32)
            cbuf = Buf()
            s.dma("sp", rc[:], self.rope_c[:, :, :], cbuf, pwrites=[cbuf])
            s.dma("sp", rs_[:], self.rope_s[:, :, :], cbuf, pwrites=[cbuf])
            s.dma("sp", gq[:].rearrange("p (h e) -> p h e", h=8),
                  bass.AP(self.qk_norm_g, l * 128, [[0, 128], [0, 8], [1, 64]]), cbuf, pwrites=[cbuf])
            s.dma("sp", gk[:].rearrange("p (h e) -> p h e", h=2),
                  bass.AP(self.qk_norm_g, l * 128 + 64, [[0, 128], [0, 2], [1, 64]]), cbuf, pwrites=[cbuf])
            s.op("dve", lambda e: e.tensor_scalar(out=gq[:], in0=gq[:], scalar1=0.125, scalar2=None, op0=ALU.mult),
                 reads=[cbuf], pwrites=[cbuf])
            f1 = self.sb(st, "p1_f1", [128, 512], F32)
            f2 = self.sb(st, "p1_f2", [128, 512], F32)
            f3 = self.sb(st, "p1_f3", [128, 512], F32)
            sm = self.sb(st, "p1_sm", [128, 3, 8], F32)
            fb = Buf()
            qst = self.sb(st, "p1_qst", [128, 4, TT], BF16)
            qsb = Buf()
            kst = self.sb(st, "p1_kst", [128, TT], BF16)
            ksb = Buf()
            tmb = self.sb(st, "p1_tmb", [128, 512], BF16)
            tmbb = Buf()

            def wload(bi):
                c0, ncol = blocks[bi][0], blocks[bi][1]
                wt, wb = wr.next()
                src = bass.AP(self.w_in, l * D * D_IN + c0, [[D_IN, 128], [128 * D_IN, KC], [1, ncol]])
                s.dma("pool", wt[:, :, 0:ncol], src, wb, writes=[wb])
                return wt, wb

            def xload(tt):
                xt, xb = xr.next()
                s.dma("sp", xt[:], self.xbf[tt], xb, writes=[xb])
                return xt, xb

            def normrope(ps_ap, psbuf, width, gt, tsub, out_bf, outbuf):
                nh = width // 64
                s.op("act", lambda e: e.activation(out=f1[:, 0:width], in_=ps_ap, func=AF.Square),
                     reads=[psbuf], writes=[fb])
                s.op("dve", lambda e: e.tensor_reduce(out=sm[:, 0, 0:nh], in_=f1[:, 0:width].rearrange("p (h e) -> p h e", e=64),
                                                      axis=AX.X, op=ALU.add), reads=[fb], pwrites=[fb])
                s.op("dve", lambda e: e.tensor_scalar(out=sm[:, 1, 0:nh], in0=sm[:, 0, 0:nh], scalar1=1.0 / 64, scalar2=RMS_EPS,
                                                      op0=ALU.mult, op1=ALU.add), reads=[fb], pwrites=[fb])
                s.op("act", lambda e: e.activation(out=sm[:, 1, 0:nh], in_=sm[:, 1, 0:nh], func=AF.Sqrt), reads=[fb], pwrites=[fb])
                s.op("dve", lambda e: e.reciprocal(out=sm[:, 2, 0:nh], in_=sm[:, 1, 0:nh]), reads=[fb], pwrites=[fb])
                s.op("dve", lambda e: e.tensor_tensor(out=f1[:, 0:width].rearrange("p (h e) -> p h e", e=64),
                                                      in0=ps_ap.rearrange("p (h e) -> p h e", e=64),
                                                      in1=sm[:, 2, 0:nh].rearrange("p (h o) -> p h o", o=1).to_broadcast([128, nh, 64]),
                                                      op=ALU.mult), reads=[psbuf, fb], pwrites=[fb])
                s.op("pool", lambda e: e.tensor_tensor(out=f1[:, 0:width], in0=f1[:, 0:width], in1=gt, op=ALU.mult),
                     reads=[fb, cbuf], pwrites=[fb])
                cb_ = rc[:, tsub, :].rearrange("p (o e) -> p o e", o=1).to_broadcast([128, nh, 64])
                sb_ = rs_[:, tsub, :].rearrange("p (o e) -> p o e", o=1).to_broadcast([128, nh, 64])
                s.op("dve", lambda e: e.tensor_tensor(out=f2[:, 0:width].rearrange("p (h e) -> p h e", e=64),
                                                      in0=f1[:, 0:width].rearrange("p (h e) -> p h e", e=64), in1=cb_, op=ALU.mult),
                     reads=[fb, cbuf], pwrites=[fb])
                sw = f1[:, 0:width].rearrange("p (h a two) -> p h a two", two=2, a=32)[:, :, :, ::-1]
                sb4 = rs_[:, tsub, :].rearrange("p (a two) -> p a two", two=2).unsqueeze(1).to_broadcast([128, nh, 32, 2])
                s.op("pool", lambda e: e.tensor_tensor(out=f3[:, 0:width].rearrange("p (h a two) -> p h a two", two=2, a=32), in0=sw,
                                                       in1=sb4, op=ALU.mult),
                     reads=[fb, cbuf], pwrites=[fb])
                s.op("dve", lambda e: e.tensor_tensor(out=out_bf, in0=f2[:, 0:width], in1=f3[:, 0:width], op=ALU.add),
                     reads=[fb], writes=[outbuf])

            ev = 0
            nxt = wload(0)
            for bi, (c0, ncol, kind, dst, scale) in enumerate(blocks):
                wt, wb = nxt
                if bi + 1 < len(blocks):
                    nxt = wload(bi + 1)
                xn = xload(0)
                for tt in range(NT):
                    xt, xb = xn
                    if tt + 1 < NT:
                        xn = xload(tt + 1)
                    for gi in range(4):
                        pt, pb = self.psb.next()
                        if kind == "fm":
                            for k in range(KC):
                                s.op("pe", lambda e: e.matmul(pt[:], lhsT=wt[:, k, gi * 128:(gi + 1) * 128], rhs=xt[:, k, :],
                                                              start=(k == 0), stop=(k == KC - 1)),
                                     reads=[wb, xb], writes=[pb] if k == 0 else (), pwrites=() if k == 0 else [pb],
                                     signal=(k == KC - 1))
                            stt, stb_ = sr.next()
                            if ev % 2 == 0:
                                s.op("act", lambda e: e.activation(out=stt[:], in_=pt[:], func=AF.Copy, scale=float(scale)),
                                     reads=[pb], writes=[stb_])
                            else:
                                s.op("dve", lambda e: e.tensor_scalar(out=stt[:], in0=pt[:], scalar1=float(scale), scalar2=None,
                                                                      op0=ALU.mult), reads=[pb], writes=[stb_])
                            ev += 1
                            s.dma("sp", self.qkT[dst + gi * 128: dst + (gi + 1) * 128, tt * TT:(tt + 1) * TT], stt[:], stb_,
                                  reads=[stb_])
                        else:
                            for k in range(KC):
                                s.op("pe", lambda e: e.matmul(pt[:, 0:ncol], lhsT=xt[:, k, gi * 128:(gi + 1) * 128], rhs=wt[:, k, 0:ncol],
                                                              start=(k == 0), stop=(k == KC - 1)),
                                     reads=[wb, xb], writes=[pb] if k == 0 else (), pwrites=() if k == 0 else [pb],
                                     signal=(k == KC - 1))
                            tsub = tt * 4 + gi
                            r0 = tt * TT + gi * 128
                            if kind == "tm":
                                stt, stb_ = sr.next()
                                if ev % 2 == 0:
                                    s.op("act", lambda e: e.activation(out=stt[:], in_=pt[:], func=AF.Copy), reads=[pb], writes=[stb_])
                                else:
                                    s.op("dve", lambda e: e.tensor_copy(out=stt[:], in_=pt[:]), reads=[pb], writes=[stb_])
                                ev += 1
                                s.dma("sp", self.vtok[r0:r0 + 128, dst:dst + 512], stt[:], stb_, reads=[stb_])
                            elif kind == "qc":
                                normrope(pt[:], pb, 512, gq[:], tsub, tmb[:], tmbb)
                                for c in range(4):
                                    s.op("pe", lambda e: e.transpose(out=self.pst[:, c * 128:(c + 1) * 128], in_=tmb[:, c * 128:(c + 1) * 128],
                                                                     identity=self.ident[:]),
                                         reads=[tmbb, self.cb], writes=[self.pstb] if c == 0 else (),
                                         pwrites=() if c == 0 else [self.pstb], signal=(c == 3))
                                s.op("act", lambda e: e.activation(out=qst[:, :, gi * 128:(gi + 1) * 128],
                                                                   in_=self.pst[:, 0:512].rearrange("p (c j) -> p c j", c=4), func=AF.Copy),
                                     reads=[self.pstb], pwrites=[qsb])
                                if gi == 3:
                                    dsta = bass.AP(self.qkT, R_QC * T + tt * TT, [[T, 128], [128 * T, 4], [1, TT]])
                                    s.dma("sp", dsta, qst[:], qsb, reads=[qsb])
                            else:
                                stt, stb_ = sr.next()
                                s.op("act", lambda e: e.activation(out=stt[:, 0:128], in_=pt[:, 128:256], func=AF.Copy),
                                     reads=[pb], writes=[stb_])
                                s.dma("sp", self.vtok[r0:r0 + 128, C_VC:C_VC + 128], stt[:, 0:128], stb_, reads=[stb_])
                                normrope(pt[:, 0:128], pb, 128, gk[:], tsub, tmb[:, 0:128], tmbb)
                                s.op("pe", lambda e: e.transpose(out=self.pst[:, 512:640], in_=tmb[:, 0:128], identity=self.ident[:]),
                                     reads=[tmbb, self.cb], writes=[self.pstb])
                                s.op("act", lambda e: e.activation(out=kst[:, gi * 128:(gi + 1) * 128], in_=self.pst[:, 512:640], func=AF.Copy),
                                     reads=[self.pstb], pwrites=[ksb])
                                if gi == 3:
                                    s.dma("sp", self.qkT[R_KC:R_KC + 128, tt * TT:(tt + 1) * TT], kst[:], ksb, reads=[ksb])
            s.barrier()

    def phase_setup(self, g):
        nc, s = self.nc, self.s
        self.bfar = self.sb(g, "bfar", [128, 8, 4], F32)
        self.ones_b = self.sb(g, "ones_b", [128, 64], BF16)
        with ExitStack() as st:
            t5 = self.sb(st, "su_t5", [32, 32], F32)
            oh = self.sb(st, "su_oh", [32, FV], F32)
            mv = self.sb(st, "su_mv", [32, FV], F32)
            fv = self.sb(st, "su_fv", [32, FV], F32)
            b = Buf()
            s.dma("sp", t5[:], self.t5_table[:, :], b, pwrites=[b])
            s.dma("sp", oh[:], self.oh_t5[:, :], b, pwrites=[b])
            s.dma("sp", mv[:], self.maskvec[:, :], b, pwrites=[b])
            for h in range(8):
                for j, bk in enumerate((15, 31)):
                    s.dma("sp", self.bfar[:, h, j:j + 1], bass.AP(self.t5_table, bk * 32 + h, [[0, 128], [1, 1]]), b, pwrites=[b])
            s.op("dve", lambda e: e.tensor_tensor(out=self.bfar[:, :, 2:4], in0=self.bfar[:, :, 0:2],
                                                  in1=self.crossb[:].unsqueeze(1).to_broadcast([128, 8, 2]), op=ALU.add),
                 reads=[self.cb], pwrites=[b])
            s.op("dve", lambda e: e.memset(self.ones_b[:], 1.0), pwrites=[b])
            fb = Buf()
            for c in range(FV // 512):
                pt, pb = self.psb.next()
                s.op("pe", lambda e: e.matmul(pt[0:32, :], lhsT=t5[:], rhs=oh[:, c * 512:(c + 1) * 512], start=True, stop=True),
                     reads=[b], writes=[pb])
                s.op("dve", lambda e: e.tensor_tensor(out=fv[:, c * 512:(c + 1) * 512], in0=pt[0:32, :], in1=mv[:, c * 512:(c + 1) * 512],
                                                      op=ALU.add), reads=[pb, b], pwrites=[fb])
            s.dma("sp", self.fvec[:, :], fv[:], fb, reads=[fb])
            s.barrier()

    def attn_begin(self, st, n_obanks, tag):
        a = {}
        a["obanks"] = [(self.ps[i], Buf()) for i in range(n_obanks)]
        a["sring"] = Ring(self.ps[n_obanks:7])
        a["s2"] = Ring([self.sb(st, tag + "_s2%d" % i, [128, 512], F32) for i in range(3)])
        a["pt"] = Ring([self.sb(st, tag + "_pt%d" % i, [128, 512], BF16) for i in range(5)])
        a["rr"] = Ring([self.sb(st, tag + "_rr%d" % i, [128, 512], F32) for i in range(2)])
        a["stg"] = Ring([self.sb(st, tag + "_st%d" % i, [128, 512], BF16) for i in range(2)])
        a["pend"] = []
        return a

    def attn_tile(self, a, k_ap, q_ap, kq_bufs, bias_ap, bias_bufs, colbias, mask_ap, mask_bufs, v_ap, v_bufs,
                  obank, first, last, kdim):
        s = self.s
        sp_, sb_ = a["sring"].next()
        s.op("pe", lambda e: e.matmul(sp_[:], lhsT=k_ap, rhs=q_ap, start=True, stop=True), reads=kq_bufs, writes=[sb_])
        pt, ptb = a["pt"].next()
        if bias_ap is not None:
            s2, s2b = a["s2"].next()
            s.op("dve", lambda e: e.tensor_tensor(out=s2[:], in0=sp_[:], in1=bias_ap, op=ALU.add),
                 reads=[sb_] + list(bias_bufs), writes=[s2b])
            s.op("act", lambda e: e.activation(out=pt[:], in_=s2[:], func=AF.Exp, bias=colbias, scale=1.0),
                 reads=[s2b, self.cb], writes=[ptb])
        else:
            s.op("act", lambda e: e.activation(out=pt[:], in_=sp_[:], func=AF.Exp, bias=colbias, scale=1.0),
                 reads=[sb_, self.cb], writes=[ptb])
        if mask_ap is not None:
            s.op("pool", lambda e: e.tensor_tensor(out=pt[:].rearrange("p (a b) -> p a b", b=64),
                                                   in0=pt[:].rearrange("p (a b) -> p a b", b=64), in1=mask_ap, op=ALU.mult),
                 reads=list(mask_bufs), pwrites=[ptb])
        a["pend"].append((pt, ptb, v_ap, v_bufs, obank, first, last))
        if len(a["pend"]) > 2:
            self.attn_pv(a)

    def attn_pv(self, a):
        s = self.s
        pt, ptb, v_ap, v_bufs, (ob, obb), first, last = a["pend"].pop(0)
        s.op("pe", lambda e: e.matmul(ob[:], lhsT=v_ap, rhs=pt[:], start=first, stop=last),
             reads=[ptb] + list(v_bufs), writes=[obb] if first else (), pwrites=() if first else [obb], signal=last)

    def attn_flush(self, a):
        while a["pend"]:
            self.attn_pv(a)

    def attn_finish_simple(self, a, obank, row0, qt):
        s = self.s
        ob, obb = obank
        rr, rrb = a["rr"].next()
        s.op("dve", lambda e: e.reciprocal(out=rr[64:128, :], in_=ob[64:128, :]), reads=[obb], writes=[rrb])
        sg, sgb = a["stg"].next()
        s.op("dve", lambda e: e.tensor_tensor(out=sg[0:64, :], in0=ob[0:64, :], in1=rr[64:128, :], op=ALU.mult),
             reads=[obb, rrb], writes=[sgb])
        s.dma("sp", self.oT[row0:row0 + 64, qt * TT:(qt + 1) * TT], sg[0:64, :], sgb, reads=[sgb])

    def vaug_ring(self, st, tag, n, ng=1):
        tiles = [self.sb(st, tag + "_va%d" % i, [128, self.KT, ng, 128], BF16) for i in range(n)]
        r = Ring(tiles)
        for t, b in zip(r.tiles, r.bufs):
            self.s.op("pool", lambda e: e.memset(t[:, :, :, 64:128], 1.0), pwrites=[b])
        return r

    def vaug_fill(self, t, b, col0s):
        for g, c0 in enumerate(col0s):
            self.s.dma("sp", t[:, :, g, 0:64], bass.AP(self.vtok, c0, [[V_COLS, 128], [128 * V_COLS, self.KT], [1, 64]]), b,
                       writes=[b] if g == 0 else (), pwrites=() if g == 0 else [b])

    def hankel(self, dst_ap, vec_handle, offset, home, pw):
        n = dst_ap.shape[-1]
        self.s.dma("sp", dst_ap, bass.AP(vec_handle, offset, [[1, dst_ap.shape[0]], [1, n]]), home, pwrites=pw)

    def phase_pc(self, l):
        nc, s = self.nc, self.s
        T, NT, KT = self.T, self.NT, self.KT
        with ExitStack() as st:
            a = self.attn_begin(st, 2, "pc")
            oi = 0
            kc = [self.sb(st, "pc_k%d" % i, [128, T], BF16) for i in range(2)]
            qb = Buf()
            for kv in range(2):
                s.op("pool", lambda e: e.memset(kc[kv][:], 0.0), pwrites=[qb])
                s.dma("sp", kc[kv][0:64, :], self.qkT[R_KC + kv * 64:R_KC + (kv + 1) * 64, :], qb, pwrites=[qb])
            qr = Ring([self.sb(st, "pc_q%d" % i, [128, T], BF16) for i in range(2)])
            for t_, b_ in zip(qr.tiles, qr.bufs):
                s.op("pool", lambda e: e.memset(t_[:], 0.0), pwrites=[b_])
            vr = self.vaug_ring(st, "pc", 2)
            for h in range(8):
                kv = h // 4
                if h % 4 == 0:
                    va, vb = vr.next()
                    self.vaug_fill(va, vb, [C_VC + kv * 64])
                qh, qhb = qr.next()
                s.dma("sp", qh[0:64, :], self.qkT[R_QC + h * 64:R_QC + (h + 1) * 64, :], qhb, pwrites=[qhb])
                for qt in range(NT):
                    for kt in range(KT):
                        cross = (kt * 128) // self.seg != (qt * TT) // self.seg
                        self.attn_tile(a, kc[kv][:, kt * 128:(kt + 1) * 128],
                                       qh[:, qt * TT:(qt + 1) * TT], [qb, qhb],
                                       None, (), self.crossb[:] if cross else self.zero_c[:], None, (),
                                       va[:, kt, 0, :], [vb], a["obanks"][oi], kt == 0, kt == KT - 1, 64)
                    self.attn_flush(a)
                    self.attn_finish_simple(a, a["obanks"][oi], 1024 + h * 64, qt)
                    oi ^= 1
            s.barrier()

    def phase_pa(self, l):
        nc, s = self.nc, self.s
        T, NT, KT = self.T, self.NT, self.KT
        with ExitStack() as st:
            a = self.attn_begin(st, 2, "pa")
            oi = 0
            qr = Ring([self.sb(st, "pa_q%d" % i, [128, T], BF16) for i in range(2)])
            kr = Ring([self.sb(st, "pa_k%d" % i, [128, T], BF16) for i in range(2)])
            for r_ in (qr, kr):
                for t_, b_ in zip(r_.tiles, r_.bufs):
                    s.op("pool", lambda e: e.memset(t_[:], 0.0), pwrites=[b_])
            vr = self.vaug_ring(st, "pa", 2)
            rm = self.sb(st, "pa_rm", [128, NT, 8, 8], F32)
            cm = self.sb(st, "pa_cm", [128, 64], F32)
            cb = Buf()
            s.dma("sp", rm[:], self.rowmask[:, :, :, :], cb, pwrites=[cb])
            s.dma("sp", cm[:], self.colmask[:, :], cb, pwrites=[cb])
            rmb = self.sb(st, "pa_rmb", [128, NT, 8, 8], BF16)
            s.op("dve", lambda e: e.tensor_copy(out=rmb[:], in_=rm[:]), reads=[cb], pwrites=[cb])
            hs = Ring([self.sb(st, "pa_hs%d" % i, [128, 512], F32) for i in range(2)])
            bt = Ring([self.sb(st, "pa_bt%d" % i, [128, 8, 512], BF16) for i in range(2)])
            for h in range(8):
                va, vb = vr.next()
                self.vaug_fill(va, vb, [C_VA + h * 64])
                bias, bb = bt.next()
                for slot in range(8):
                    ht, hb = hs.next()
                    for krl in range(2):
                        off = ((l * 8 + h) * 31 + (2 * slot + 4 + krl)) * 127
                        s.dma("sp", ht[krl * 64:(krl + 1) * 64, :].rearrange("p (a b) -> p a b", b=64),
                              bass.AP(self.na_rpb, off, [[1, 64], [127, 8], [1, 64]]), hb,
                              writes=[hb] if krl == 0 else (), pwrites=() if krl == 0 else [hb])
                    s.op("dve", lambda e: e.tensor_tensor(out=bias[:, slot, :].rearrange("p (a b) -> p a b", b=64),
                                                          in0=ht[:, ::-1].rearrange("p (a b) -> p a b", b=64),
                                                          in1=cm[:].unsqueeze(1).to_broadcast([128, 8, 64]), op=ALU.add),
                         reads=[hb, cb], writes=[bb] if slot == 0 else (), pwrites=() if slot == 0 else [bb])
                qh, qhb = qr.next()
                kh, khb = kr.next()
                s.dma("sp", qh[0:64, :], self.qkT[R_QA + h * 64:R_QA + (h + 1) * 64, :], qhb, pwrites=[qhb])
                s.dma("sp", kh[0:64, :], self.qkT[R_KA + h * 64:R_KA + (h + 1) * 64, :], khb, pwrites=[khb])
                for qt in range(NT):
                    kts = [kt for kt in range(4 * qt - 2, 4 * qt + 6) if 0 <= kt < KT]
                    for i, kt in enumerate(kts):
                        slot = kt - 4 * qt + 2
                        self.attn_tile(a, kh[:, kt * 128:(kt + 1) * 128], qh[:, qt * TT:(qt + 1) * TT], [qhb, khb],
                                       bias[:, slot, :], [bb], self.zero_c[:],
                                       rmb[:, qt, slot, :].unsqueeze(2).to_broadcast([128, 8, 64]), [cb],
                                       va[:, kt, 0, :], [vb], a["obanks"][oi], i == 0, i == len(kts) - 1, 64)
                    self.attn_flush(a)
                    self.attn_finish_simple(a, a["obanks"][oi], 0 + h * 64, qt)
                    oi ^= 1
            s.barrier()

    def phase_pd(self, l):
        nc, s = self.nc, self.s
        T, NT, KT = self.T, self.NT, self.KT
        with ExitStack() as st:
            a = self.attn_begin(st, 2, "pd")
            oi = 0
            qd = self.sb(st, "pd_q", [128, 3, T], BF16)
            kd = self.sb(st, "pd_k", [128, 3, T], BF16)
            qb = Buf()
            s.op("pool", lambda e: e.memset(qd[:], 0.0), pwrites=[qb])
            s.op("pool", lambda e: e.memset(kd[:], 0.0), pwrites=[qb])
            vr = self.vaug_ring(st, "pd", 1, 3)
            va, vb = vr.next()
            ranges = [(-1, 4), (-2, 5), (-8, 11)]
            ntile = sum(hi - lo + 1 for lo, hi in ranges)
            bias = self.sb(st, "pd_bias", [128, ntile, 512], BF16)
            bb = Buf()
            hs = Ring([self.sb(st, "pd_hs%d" % i, [128, 512], F32) for i in range(2)])
            for sl in range(8):
                for g in range(3):
                    r0 = R_QD + (g * 8 + sl) * 64
                    s.dma("sp", qd[0:64, g, :], self.qkT[r0:r0 + 64, :], qb, pwrites=[qb])
                    r0 = R_KD + (g * 8 + sl) * 64
                    s.dma("sp", kd[0:64, g, :], self.qkT[r0:r0 + 64, :], qb, pwrites=[qb])
                self.vaug_fill(va, vb, [C_VD + (g * 8 + sl) * 64 for g in range(3)])
                ti = 0
                tidx = {}
                for g in range(3):
                    lo, hi = ranges[g]
                    for o in range(lo, hi + 1):
                        ht, hb = hs.next()
                        s.dma("sp", ht[:], bass.AP(self.fvec, (8 + 8 * g + sl) * FV + 1025 + o * 128, [[1, 128], [1, 512]]), hb, writes=[hb])
                        s.op("pool" if ti % 2 else "dve", lambda e: e.tensor_copy(out=bias[:, ti, :], in_=ht[:, ::-1]),
                             reads=[hb], writes=[bb] if ti == 0 else (), pwrites=() if ti == 0 else [bb])
                        tidx[(g, o)] = ti
                        ti += 1
                for qt in range(NT):
                    tl = []
                    for g in range(3):
                        lo, hi = ranges[g]
                        for o in range(lo, hi + 1):
                            kt = 4 * qt + o
                            if 0 <= kt < KT:
                                tl.append((g, o, kt))
                    for i, (g, o, kt) in enumerate(tl):
                        cross = (kt * 128) // self.seg != (qt * TT) // self.seg
                        self.attn_tile(a, kd[:, g, kt * 128:(kt + 1) * 128], qd[:, g, qt * TT:(qt + 1) * TT], [qb],
                                       bias[:, tidx[(g, o)], :], [bb], self.crossb[:] if cross else self.zero_c[:], None, (),
                                       va[:, kt, g, :], [vb], a["obanks"][oi], i == 0, i == len(tl) - 1, 64)
                    self.attn_flush(a)
                    self.attn_finish_simple(a, a["obanks"][oi], 1536 + sl * 64, qt)
                    oi ^= 1
            s.barrier()

    def phase_pb(self, l):
        nc, s = self.nc, self.s
        T, NT, KT = self.T, self.NT, self.KT
        lam_init = 0.8 - 0.6 * math.exp(-0.3 * l)
        with ExitStack() as st:
            a = self.attn_begin(st, 4, "pb")
            oi = 0
            lq = self.sb(st, "pb_lq", [128, 128], F32)
            lt = self.sb(st, "pb_lt", [128, 8], F32)
            gcol = self.sb(st, "pb_g", [64, 1], F32)
            lb = Buf()
            s.dma("sp", lq[:], bass.AP(self.diff_lambda, l * 128, [[0, 128], [1, 128]]), lb, pwrites=[lb])
            s.dma("sp", gcol[:], bass.AP(self.diff_subln_g, l * 64, [[1, 64], [1, 1]]), lb, pwrites=[lb])
            s.op("dve", lambda e: e.tensor_tensor(out=lq[:, 0:32], in0=lq[:, 0:32], in1=lq[:, 32:64], op=ALU.mult), reads=[lb], pwrites=[lb])
            s.op("dve", lambda e: e.tensor_tensor(out=lq[:, 64:96], in0=lq[:, 64:96], in1=lq[:, 96:128], op=ALU.mult), reads=[lb], pwrites=[lb])
            s.op("dve", lambda e: e.tensor_reduce(out=lt[:, 0:1], in_=lq[:, 0:32], axis=AX.X, op=ALU.add), reads=[lb], pwrites=[lb])
            s.op("dve", lambda e: e.tensor_reduce(out=lt[:, 1:2], in_=lq[:, 64:96], axis=AX.X, op=ALU.add), reads=[lb], pwrites=[lb])
            s.op("act", lambda e: e.activation(out=lt[:, 2:4], in_=lt[:, 0:2], func=AF.Exp), reads=[lb], pwrites=[lb])
            s.op("dve", lambda e: e.tensor_tensor(out=lt[:, 4:5], in0=lt[:, 3:4], in1=lt[:, 2:3], op=ALU.subtract), reads=[lb], pwrites=[lb])
            s.op("dve", lambda e: e.tensor_scalar(out=lt[:, 5:6], in0=lt[:, 4:5], scalar1=-lam_init, scalar2=None, op0=ALU.add), reads=[lb], pwrites=[lb])
            s.op("dve", lambda e: e.tensor_scalar(out=gcol[:], in0=gcol[:], scalar1=1.0 - lam_init, scalar2=None, op0=ALU.mult), reads=[lb], pwrites=[lb])
            nlam = lt[0:64, 5:6]
            vr = self.vaug_ring(st, "pb", 2)
            qr = Ring([self.sb(st, "pb_q%d" % i, [128, 2, T], BF16) for i in range(2)])
            kr = Ring([self.sb(st, "pb_k%d" % i, [128, 2, T], BF16) for i in range(2)])
            for r_ in (qr, kr):
                for t_, b_ in zip(r_.tiles, r_.bufs):
                    s.op("pool", lambda e: e.memset(t_[:], 0.0), pwrites=[b_])
            hs = Ring([self.sb(st, "pb_hs%d" % i, [128, 512], F32) for i in range(2)])
            bt = Ring([self.sb(st, "pb_bt%d" % i, [128, 6, 512], BF16) for i in range(2)])
            rr = self.sb(st, "pb_rr", [128, 2, 512], F32)
            od = self.sb(st, "pb_od", [64, 3, 512], F32)
            wb_ = Buf()
            for h in range(8):
                qh, qhb = qr.next()
                kh, khb = kr.next()
                va, vb = vr.next()
                self.vaug_fill(va, vb, [C_VB + h * 64])
                for c in range(2):
                    r0 = R_QB + h * 64 + c * 32
                    s.dma("sp", qh[0:32, c, :], self.qkT[r0:r0 + 32, :], qhb, pwrites=[qhb])
                    r0 = R_KB + h * 64 + c * 32
                    s.dma("sp", kh[0:32, c, :], self.qkT[r0:r0 + 32, :], khb, pwrites=[khb])
                bias, bb = bt.next()
                for bi_, o in enumerate(range(-1, 5)):
                    ht, hb = hs.next()
                    s.dma("sp", ht[:], bass.AP(self.fvec, h * FV + 1025 + o * 128, [[1, 128], [1, 512]]), hb, writes=[hb])
                    s.op("pool", lambda e: e.tensor_copy(out=bias[:, bi_, :], in_=ht[:, ::-1]),
                         reads=[hb], writes=[bb] if bi_ == 0 else (), pwrites=() if bi_ == 0 else [bb])
                for qt in range(NT):
                    for c in range(2):
                        for kt in range(KT):
                            o = kt - 4 * qt
                            cross = (kt * 128) // self.seg != (qt * TT) // self.seg
                            near = -1 <= o <= 4
                            if near:
                                colb = self.crossb[:] if cross else self.zero_c[:]
                                bap, bbufs = bias[:, o + 1, :], [bb]
                            else:
                                j = (0 if o < 0 else 1) + (2 if cross else 0)
                                colb = self.bfar[:, h, j:j + 1]
                                bap, bbufs = None, ()
                            self.attn_tile(a, kh[:, c, kt * 128:(kt + 1) * 128], qh[:, c, qt * TT:(qt + 1) * TT], [qhb, khb],
                                           bap, bbufs, colb, None, (), va[:, kt, 0, :], [vb], a["obanks"][oi + c],
                                           kt == 0, kt == KT - 1, 32)
                        self.attn_flush(a)
                    for c in range(2):
                        ob, obb = a["obanks"][oi + c]
                        s.op("dve", lambda e: e.reciprocal(out=rr[64:128, c, :], in_=ob[64:128, :]), reads=[obb], pwrites=[wb_])
                        s.op("dve", lambda e: e.tensor_tensor(out=od[:, c, :], in0=ob[0:64, :], in1=rr[64:128, c, :], op=ALU.mult),
                             reads=[obb, wb_], pwrites=[wb_])
                    oi ^= 2
                    s.op("dve", lambda e: e.scalar_tensor_tensor(out=od[:, 2, :], in0=od[:, 1, :], scalar=nlam, in1=od[:, 0, :],
                                                                 op0=ALU.mult, op1=ALU.add), reads=[wb_, lb], pwrites=[wb_])
                    s.op("pool", lambda e: e.tensor_tensor(out=od[:, 0, :], in0=od[:, 2, :], in1=od[:, 2, :], op=ALU.mult), reads=[wb_], pwrites=[wb_])
                    mp, mpb = a["sring"].next()
                    s.op("pe", lambda e: e.matmul(mp[0:64, :], lhsT=self.ones_f[0:64, 0:64], rhs=od[:, 0, :], start=True, stop=True),
                         reads=[wb_, self.cb], writes=[mpb])
                    s.op("dve", lambda e: e.tensor_scalar(out=od[:, 1, :], in0=mp[0:64, :], scalar1=1.0 / 64, scalar2=RMS_EPS,
                                                          op0=ALU.mult, op1=ALU.add), reads=[mpb], pwrites=[wb_])
                    s.op("act", lambda e: e.activation(out=od[:, 1, :], in_=od[:, 1, :], func=AF.Sqrt), reads=[wb_], pwrites=[wb_])
                    s.op("dve", lambda e: e.reciprocal(out=od[:, 0, :], in_=od[:, 1, :]), reads=[wb_], pwrites=[wb_])
                    s.op("dve", lambda e: e.tensor_tensor(out=od[:, 1, :], in0=od[:, 2, :], in1=od[:, 0, :], op=ALU.mult), reads=[wb_], pwrites=[wb_])
                    sg, sgb = a["stg"].next()
                    s.op("dve", lambda e: e.tensor_scalar(out=sg[0:64, :], in0=od[:, 1, :], scalar1=gcol[:, 0:1], scalar2=None, op0=ALU.mult),
                         reads=[wb_, lb], writes=[sgb])
                    s.dma("sp", self.oT[512 + h * 64:512 + (h + 1) * 64, qt * TT:(qt + 1) * TT], sg[0:64, :], sgb, reads=[sgb])
            s.barrier()

    def phase_p2a(self, l):
        nc, s = self.nc, self.s
        T, NT = self.T, self.NT
        with ExitStack() as st0:
            gz = self.sb(st0, "p2_gz", [128, 2, T], BF16)
            gzb = Buf()
            with ExitStack() as st:
                wgd = self.sb(st, "p2_wgd", [128, KC, 256], BF16)
                wb = Buf()
                s.dma("pool", wgd[:], bass.AP(self.w_gd, l * D * 256, [[256, 128], [128 * 256, KC], [1, 256]]), wb, writes=[wb])
                xr = Ring([self.sb(st, "p2_x%d" % i, [128, KC, TT], BF16) for i in range(2)])
                for tt in range(NT):
                    xt, xb = xr.next()
                    s.dma("sp", xt[:], self.xbf[tt], xb, writes=[xb])
                    for c in range(2):
                        pt, pb = self.psb.next()
                        for k in range(KC):
                            s.op("pe", lambda e: e.matmul(pt[:], lhsT=wgd[:, k, c * 128:(c + 1) * 128], rhs=xt[:, k, :], start=(k == 0), stop=(k == KC - 1)),
                                 reads=[wb, xb], writes=[pb] if k == 0 else (), pwrites=() if k == 0 else [pb], signal=(k == KC - 1))
                        s.op("act", lambda e: e.activation(out=gz[:, c, tt * TT:(tt + 1) * TT], in_=pt[:], func=AF.Copy), reads=[pb], pwrites=[gzb])
                s.barrier()
            with ExitStack() as st:
                wgr = Ring([self.sb(st, "p2_wgu%d" % i, [128, 2, 4, 512], BF16) for i in range(2)])
                wbr = Ring([self.sb(st, "p2_wbr%d" % i, [128, 16, 512], BF16) for i in range(2)])
                orr = Ring([self.sb(st, "p2_o%d" % i, [128, 16, TT], BF16) for i in range(2)])
                bg = self.sb(st, "p2_bg", [128, 4 * KC], F32)
                bgb = Buf()
                s.dma("sp", bg[:], self.b_gate[l], bgb, writes=[bgb])
                gts = Ring([self.sb(st, "p2_gt%d" % i, [128, 512], F32) for i in range(4)])
                tms = Ring([self.sb(st, "p2_tm%d" % i, [128, 512], F32) for i in range(4)])
                m1 = Ring([self.sb(st, "p2_m%d" % i, [128, 512], F32) for i in range(2)])
                sr = Ring([self.sb(st, "p2_s%d" % i, [128, 512], BF16) for i in range(3)])

                def wload(db):
                    w1, b1 = wgr.next()
                    s.dma("pool", w1[:, 0, :, :], bass.AP(self.w_gu, l * 256 * 4 * D + db * 512, [[4 * D, 128], [D, 4], [1, 512]]), b1, writes=[b1])
                    s.dma_more("pool", w1[:, 1, :, :], bass.AP(self.w_gu, l * 256 * 4 * D + 128 * 4 * D + db * 512, [[4 * D, 128], [D, 4], [1, 512]]), b1, b1)
                    w2, b2 = wbr.next()
                    s.dma("pool", w2[:], bass.AP(self.w_br, l * 2048 * D + db * 512, [[D, 128], [128 * D, 16], [1, 512]]), b2, writes=[b2])
                    return w1, b1, w2, b2

                nxt = wload(0)
                for db in range(D // 512):
                    w1, b1, w2, b2 = nxt
                    if db + 1 < D // 512:
                        nxt = wload(db + 1)
                    for tt in range(NT):
                        ot, ob = orr.next()
                        s.dma("sp", ot[:], bass.AP(self.oT, tt * TT, [[T, 128], [128 * T, 16], [1, TT]]), ob, writes=[ob])
                        for dc in range(4):
                            dchunk = db * 4 + dc
                            tl = []
                            for n in range(4):
                                pg, pgb = self.psb.next()
                                for k in range(2):
                                    s.op("pe", lambda e: e.matmul(pg[:], lhsT=w1[:, k, n, dc * 128:(dc + 1) * 128], rhs=gz[:, k, tt * TT:(tt + 1) * TT],
                                                                  start=(k == 0), stop=(k == 1)),
                                         reads=[b1, gzb], writes=[pgb] if k == 0 else (), pwrites=() if k == 0 else [pgb], signal=(k == 1))
                                gt, gtb = gts.next()
                                s.op("act", lambda e: e.activation(out=gt[:], in_=pg[:], func=AF.Sigmoid,
                                                                   bias=bg[:, n * KC + dchunk:n * KC + dchunk + 1], scale=1.0),
                                     reads=[pgb, bgb], writes=[gtb])
                                po, pob = self.psb.next()
                                for k in range(4):
                                    s.op("pe", lambda e: e.matmul(po[:], lhsT=w2[:, n * 4 + k, dc * 128:(dc + 1) * 128], rhs=ot[:, n * 4 + k, :],
                                                                  start=(k == 0), stop=(k == 3)),
                                         reads=[b2, ob], writes=[pob] if k == 0 else (), pwrites=() if k == 0 else [pob], signal=(k == 3))
                                tm, tmb = tms.next()
                                s.op("dve", lambda e: e.tensor_tensor(out=tm[:], in0=po[:], in1=gt[:], op=ALU.mult), reads=[pob, gtb], writes=[tmb])
                                tl.append((tm, tmb))
                            ma, mab = m1.next()
                            s.op("pool", lambda e: e.tensor_tensor(out=ma[:], in0=tl[0][0][:], in1=tl[1][0][:], op=ALU.add),
                                 reads=[tl[0][1], tl[1][1]], writes=[mab])
                            mb_, mbb = m1.next()
                            s.op("pool", lambda e: e.tensor_tensor(out=mb_[:], in0=tl[2][0][:], in1=tl[3][0][:], op=ALU.add),
                                 reads=[tl[2][1], tl[3][1]], writes=[mbb])
                            sg, sgb = sr.next()
                            s.op("pool", lambda e: e.tensor_tensor(out=sg[:], in0=ma[:], in1=mb_[:], op=ALU.add), reads=[mab, mbb], writes=[sgb])
                            s.dma("sp", self.mrg[tt, :, dchunk, :], sg[:], sgb, reads=[sgb])
                s.barrier()

    def phase_lin_res(self, wh, woff, kchunks, src, pieces, ncols_blk, tag):
        nc, s = self.nc, self.s
        NT = self.NT
        nb = D // ncols_blk
        nck = ncols_blk // 128
        with ExitStack() as st:
            wr = Ring([self.sb(st, tag + "_w%d" % i, [128, kchunks, ncols_blk], BF16) for i in range(2)])
            npiece = len(pieces)
            pmax = max(pieces)
            ar = Ring([self.sb(st, tag + "_a%d" % i, [128, pmax, TT], BF16) for i in range(3 if npiece > 1 else 2)])
            xr = Ring([self.sb(st, tag + "_xr%d" % i, [128, nck, TT], F32) for i in range(2)])
            yr = Ring([self.sb(st, tag + "_y%d" % i, [128, TT], F32) for i in range(4)])
            pstart = [sum(pieces[:i]) for i in range(npiece)]

            def wload(b):
                wt, wb = wr.next()
                s.dma("pool", wt[:], bass.AP(wh, woff + b * ncols_blk, [[D, 128], [128 * D, kchunks], [1, ncols_blk]]), wb, writes=[wb])
                return wt, wb

            def aload(tt, pi):
                at, ab = ar.next()
                s.dma("sp", at[:, 0:pieces[pi], :], src[tt, :, pstart[pi]:pstart[pi] + pieces[pi], :], ab, writes=[ab])
                return at, ab

            seq = [(b, tt, pi) for b in range(nb) for tt in range(NT) for pi in range(npiece)]
            nxt_w = wload(0)
            nxt_a = aload(0, 0)
            ai = 0
            for b in range(nb):
                wt, wb = nxt_w
                if b + 1 < nb:
                    nxt_w = wload(b + 1)
                for tt in range(NT):
                    xt, xb = xr.next()
                    s.dma("sp", xt[:], self.xres[tt, :, b * nck:(b + 1) * nck, :], xb, writes=[xb])
                    banks = [self.psb.next() for _ in range(nck)]
                    for pi in range(npiece):
                        at, ab = nxt_a
                        ai += 1
                        if ai < len(seq):
                            nxt_a = aload(seq[ai][1], seq[ai][2])
                        for c in range(nck):
                            pt, pb = banks[c]
                            for k in range(pieces[pi]):
                                kk = pstart[pi] + k
                                first, lastk = (kk == 0), (kk == kchunks - 1)
                                s.op("pe", lambda e: e.matmul(pt[:], lhsT=wt[:, kk, c * 128:(c + 1) * 128], rhs=at[:, k, :], start=first, stop=lastk),
                                     reads=[wb, ab], writes=[pb] if first else (), pwrites=() if first else [pb],
                                     signal=(lastk or k == pieces[pi] - 1))
                    for c in range(nck):
                        pt, pb = banks[c]
                        yt, yb = yr.next()
                        s.op("dve", lambda e: e.scalar_tensor_tensor(out=yt[:], in0=xt[:, c, :], scalar=float(ALPHA), in1=pt[:],
                                                                     op0=ALU.mult, op1=ALU.add), reads=[xb, pb], writes=[yb])
                        s.dma("sp", self.ypre[tt, :, b * nck + c, :], yt[:], yb, reads=[yb])
            s.barrier()

    def phase_p3a(self, l):
        nc, s = self.nc, self.s
        NT = self.NT
        FB = 256
        nb = D_FF // FB
        with ExitStack() as st:
            wr = Ring([self.sb(st, "p3_w%d" % i, [128, KC, 2, FB], BF16) for i in range(2)])
            xr = Ring([self.sb(st, "p3_x%d" % i, [128, KC, TT], BF16) for i in range(2)])
            sgr = Ring([self.sb(st, "p3_g%d" % i, [128, TT], F32) for i in range(3)])
            sr = Ring([self.sb(st, "p3_s%d" % i, [128, TT], BF16) for i in range(4)])

            def wload(b):
                wt, wb = wr.next()
                s.dma("pool", wt[:, :, 0, :], bass.AP(self.w_fi, l * D * 2 * D_FF + b * FB, [[2 * D_FF, 128], [128 * 2 * D_FF, KC], [1, FB]]), wb, writes=[wb])
                s.dma_more("pool", wt[:, :, 1, :], bass.AP(self.w_fi, l * D * 2 * D_FF + D_FF + b * FB, [[2 * D_FF, 128], [128 * 2 * D_FF, KC], [1, FB]]), wb, wb)
                return wt, wb

            def xload(tt):
                xt, xb = xr.next()
                s.dma("sp", xt[:], self.xbf[tt], xb, writes=[xb])
                return xt, xb

            nxt = wload(0)
            for b in range(nb):
                wt, wb = nxt
                if b + 1 < nb:
                    nxt = wload(b + 1)
                xn = xload(0)
                for tt in range(NT):
                    xt, xb = xn
                    if tt + 1 < NT:
                        xn = xload(tt + 1)
                    for c in range(FB // 128):
                        pg, pgb = self.psb.next()
                        pu, pub = self.psb.next()
                        for j, (pt, pb) in enumerate(((pg, pgb), (pu, pub))):
                            for k in range(KC):
                                s.op("pe", lambda e: e.matmul(pt[:], lhsT=wt[:, k, j, c * 128:(c + 1) * 128], rhs=xt[:, k, :], start=(k == 0), stop=(k == KC - 1)),
                                     reads=[wb, xb], writes=[pb] if k == 0 else (), pwrites=() if k == 0 else [pb], signal=(k == KC - 1))
                        sg, sgb = sgr.next()
                        s.op("act", lambda e: e.activation(out=sg[:], in_=pg[:], func=AF.Silu), reads=[pgb], writes=[sgb])
                        so, sob = sr.next()
                        s.op("dve", lambda e: e.tensor_tensor(out=so[:], in0=pu[:], in1=sg[:], op=ALU.mult), reads=[pub, sgb], writes=[sob])
                        s.dma("sp", self.ffT[tt, :, b * (FB // 128) + c, :], so[:], sob, reads=[sob])
            s.barrier()

    def layer(self, l):
        stop = [d for d in self.debug if d.startswith("stop_")]
        stop = stop[0] if stop else None
        self.phase_p1(l)
        self.phase_pa(l)
        self.phase_pb(l)
        self.phase_pc(l)
        self.phase_pd(l)
        self.phase_p2a(l)
        if stop == "stop_p2a":
            return True
        self.phase_lin_res(self.w_out, l * D * D, KC, self.mrg, [KC], 512, "p2b")
        if stop == "stop_p2b":
            return True
        self.phase_ln(self.ypre, l, self.ln1_g, self.ln1_b, None, last=False)
        if stop == "stop_ln1":
            return True
        self.phase_p3a(l)
        if stop == "stop_p3a":
            return True
        self.phase_lin_res(self.w_fo, l * D_FF * D, D_FF // 128, self.ffT, [22, 22, 21, 21], 256, "p3b")
        if stop == "stop_p3b":
            return True
        self.phase_ln(self.ypre, l, self.ln2_g, self.ln2_b, None, last=(l == self.L - 1))
        return False


def core_inputs(inp, x_tokens, seg, cross, n_layers):
    L = n_layers
    c = make_consts(seg, cross)
    rpb = np.zeros((L, 8, 31, 127), np.float32)
    rpb[:, :, 8:23, 48:79] = np.asarray(inp["na_rpb"], np.float32)[:L]
    f32 = lambda a: np.ascontiguousarray(np.asarray(a, np.float32))
    m = {
        "xin": tile_major(f32(x_tokens)),
        "ln_in_g": cols(inp["ln_in_g"]), "ln_in_b": cols(inp["ln_in_b"]),
        "w_in": f32(inp["w_in"][:L]),
        "na_rpb": rpb,
        "qk_norm_g": f32(inp["qk_norm_g"][:L]),
        "diff_lambda": f32(inp["diff_lambda"][:L]).reshape(L, 128),
        "diff_subln_g": f32(inp["diff_subln_g"][:L]),
        "t5_table": f32(inp["t5_table"]),
        "w_gate_down": f32(inp["w_gate_down"][:L]),
        "w_gate_up": f32(inp["w_gate_up"][:L]),
        "b_gate": np.stack([np.concatenate([cols(inp["b_gate"][l][n]) for n in range(4)], axis=1) for l in range(L)]),
        "w_branch": f32(inp["w_branch"][:L]).reshape(L, 2048, D),
        "w_out": f32(inp["w_out"][:L]),
        "ln1_g": np.stack([cols(inp["ln1_g"][l]) for l in range(L)]),
        "ln1_b": np.stack([cols(inp["ln1_b"][l]) for l in range(L)]),
        "w_ffn_in": f32(inp["w_ffn_in"][:L]),
        "w_ffn_out": f32(inp["w_ffn_out"][:L]),
        "ln2_g": np.stack([cols(inp["ln2_g"][l]) for l in range(L)]),
        "ln2_b": np.stack([cols(inp["ln2_b"][l]) for l in range(L)]),
    }
    m.update(c)
    return m


_PROG_CACHE = {}


def kernel(**inputs):
    seg = 2048
    L = DEPTH
    xp = np.asarray(inputs["x_prompt"], np.float32)
    xs = np.asarray(inputs["x_sample"], np.float32)
    if "prog" not in _PROG_CACHE:
        _PROG_CACHE["prog"] = Prog(L, seg).build()
    nc = _PROG_CACHE["prog"]
    shared = None
    in_maps = []
    for c in range(8):
        if c < 4:
            xt, cross = xp[c], True
        else:
            xt, cross = np.concatenate([xs[2 * (c - 4)], xs[2 * (c - 4) + 1]], axis=0), False
        if shared is None:
            m = core_inputs(inputs, xt, seg, cross, L)
            shared = m
        else:
            m = dict(shared)
            m["xin"] = tile_major(np.ascontiguousarray(xt, dtype=np.float32))
            m.update(make_consts(seg, cross))
        in_maps.append(m)
    res = run_bass_kernel_spmd(nc, in_maps, core_ids=list(range(8)))
    y_prompt = np.empty_like(xp)
    y_sample = np.empty_like(xs)
    for c in range(8):
        y = untile_major(np.asarray(res.results[c]["yout"]))
        if c < 4:
            y_prompt[c] = y
        else:
            y_sample[2 * (c - 4)] = y[:seg]
            y_sample[2 * (c - 4) + 1] = y[seg:]
    return (y_prompt, y_sample)
```

```python
import math
from contextlib import ExitStack

import numpy as np
import ml_dtypes

import concourse.bass as bass
import concourse.mybir as mybir
from concourse.bass_utils import run_bass_kernel_spmd

F32 = mybir.dt.float32
BF16 = mybir.dt.bfloat16
AF = mybir.ActivationFunctionType
ALU = mybir.AluOpType
AX = mybir.AxisListType

D = 4096
DEPTH = 4
HD = 64
D_IN = 8448
D_FF = 11008
KC = D // 128
TT = 512
NEG = -30000.0
ALPHA = (2 * DEPTH) ** 0.25
LN_EPS = 1e-5
RMS_EPS = 1e-6
DIL = (1, 4, 16)
FV = 3072

R_QA, R_KA, R_QB, R_KB, R_QD, R_KD, R_QC, R_KC = 0, 512, 1024, 1536, 2048, 3584, 5120, 5632
QK_ROWS = 5760
C_VA, C_VB, C_VC, C_VD = 0, 512, 1024, 1152
V_COLS = 2688


class Buf:
    __slots__ = ("w", "r", "dsem", "name")

    def __init__(self, name=""):
        self.w = {}
        self.r = {}
        self.dsem = None
        self.name = name


class Sched:
    def __init__(self, nc, n_dsem=96):
        self.nc = nc
        self.eng = {"pe": nc.tensor, "act": nc.scalar, "dve": nc.vector, "pool": nc.gpsimd, "sp": nc.sync}
        self.semh = []
        self.psem = {}
        self.pcnt = {}
        for k in ("pe", "act", "dve", "pool"):
            self.psem[k] = self._new_sem("prog_" + k)
            self.pcnt[k] = 0
        self.free_dsems = [self._new_sem("dma%d" % i) for i in range(n_dsem)]
        self.dcount = {s: 0 for s in self.free_dsems}
        self.waited = {k: {} for k in self.eng}
        self.latest = {}
        self.phase_dsems = []
        self.pe_pending = False
        self.swq = []

    def _new_sem(self, name):
        self.semh.append(self.nc.alloc_semaphore(name=name))
        return len(self.semh) - 1

    def _deps(self, e, reads, writes, pwrites):
        need = {}
        for b in reads:
            for s, v in b.w.items():
                if need.get(s, 0) < v:
                    need[s] = v
        for bl in (writes, pwrites):
            for b in bl:
                for d in (b.w, b.r):
                    for s, v in d.items():
                        if need.get(s, 0) < v:
                            need[s] = v
        own = self.psem["pe"] if e == "pe" else None
        wd = self.waited[e]
        en = self.eng[e]
        for s, v in need.items():
            if s == own:
                continue
            if wd.get(s, 0) < v:
                en.wait_ge(self.semh[s], v)
                wd[s] = v

    def _mark(self, s, val, reads, writes, pwrites):
        for b in reads:
            if b.r.get(s, 0) < val:
                b.r[s] = val
        for b in writes:
            b.w = {s: val}
            b.r = {}
        for b in pwrites:
            if b.w.get(s, 0) < val:
                b.w[s] = val
        if self.latest.get(s, 0) < val:
            self.latest[s] = val

    def op(self, e, fn, reads=(), writes=(), pwrites=(), signal=True):
        self._deps(e, reads, writes, pwrites)
        ins = fn(self.eng[e])
        s = self.psem[e]
        if signal:
            self.pcnt[e] += 1
            ins.then_inc(self.semh[s], 1)
            val = self.pcnt[e]
            if e == "pe":
                self.pe_pending = False
        else:
            assert e == "pe"
            val = self.pcnt[e] + 1
            self.pe_pending = True
        self._mark(s, val, reads, writes, pwrites)
        return ins

    def _sw_throttle(self, q, out, s):
        if q != "pool":
            return
        nd = 2
        for d in out.shape[:-1]:
            nd *= d
        nd = nd // 16 + 4
        wd = self.waited["pool"]
        while self.swq and sum(x[2] for x in self.swq) + nd > 900:
            s0, v0, _ = self.swq.pop(0)
            if wd.get(s0, 0) < v0:
                self.eng["pool"].wait_ge(self.semh[s0], v0)
                wd[s0] = v0
        self.swq.append((s, self.dcount[s] + 16, nd))

    def dma(self, q, out, in_, home, reads=(), writes=(), pwrites=()):
        self._deps(q, reads, writes, pwrites)
        if home.dsem is None:
            home.dsem = self.free_dsems.pop()
            self.phase_dsems.append(home.dsem)
        s = home.dsem
        self._sw_throttle(q, out, s)
        ins = self.eng[q].dma_start(out=out, in_=in_)
        self.dcount[s] += 16
        ins.then_inc(self.semh[s], 16)
        self._mark(s, self.dcount[s], reads, writes, pwrites)
        return ins

    def dma_more(self, q, out, in_, home, buf):
        s = home.dsem
        self._sw_throttle(q, out, s)
        ins = self.eng[q].dma_start(out=out, in_=in_)
        self.dcount[s] += 16
        ins.then_inc(self.semh[s], 16)
        buf.w[s] = self.dcount[s]
        self.latest[s] = self.dcount[s]

    def barrier(self):
        assert not self.pe_pending
        for e in self.eng:
            wd = self.waited[e]
            for s, v in self.latest.items():
                if e == "pe" and s == self.psem["pe"]:
                    continue
                if wd.get(s, 0) < v:
                    self.eng[e].wait_ge(self.semh[s], v)
                    wd[s] = v
        self.free_dsems.extend(self.phase_dsems)
        self.phase_dsems = []


class Ring:
    def __init__(self, tiles):
        self.tiles = tiles
        self.bufs = [Buf() for _ in tiles]
        self.i = -1

    def next(self):
        self.i = (self.i + 1) % len(self.tiles)
        return self.tiles[self.i], self.bufs[self.i]


def _t5_bucket_np(rel):
    import jax
    import jax.numpy as jnp
    with jax.default_device(jax.devices("cpu")[0]):
        rel = jnp.asarray(rel, dtype=jnp.int32)
        half = 16
        max_exact = 8
        n = jnp.abs(rel)
        nf = jnp.maximum(n, 1).astype(jnp.float32)
        large = max_exact + (jnp.log(nf / max_exact) / math.log(128 / max_exact)
                             * (half - max_exact)).astype(jnp.int32)
        large = jnp.minimum(large, half - 1)
        out = jnp.where(rel > 0, half, 0) + jnp.where(n < max_exact, n, large)
        return np.asarray(out)


def make_consts(seg, cross):
    T = 2 * seg
    pos = np.arange(T) if cross else (np.arange(T) % seg)
    freqs = (10000.0 ** (-np.arange(0, 32, 2, dtype=np.float32) / np.float32(32))).astype(np.float32)
    row = (pos // 64).astype(np.float32)
    col = (pos % 64).astype(np.float32)
    ang = np.concatenate([row[:, None] * freqs, col[:, None] * freqs], axis=-1).astype(np.float32)
    c, s = np.cos(ang).astype(np.float32), np.sin(ang).astype(np.float32)
    cc = np.repeat(c, 2, axis=1)
    ss = np.repeat(s, 2, axis=1)
    ss[:, 0::2] *= -1.0
    rope_c = np.ascontiguousarray(cc.reshape(T // 128, 128, 64).transpose(1, 0, 2))
    rope_s = np.ascontiguousarray(ss.reshape(T // 128, 128, 64).transpose(1, 0, 2))
    nqt = T // 512
    rs = (T // 64) if cross else (seg // 64)
    rm = np.zeros((128, nqt, 8, 8), np.float32)
    for j in range(nqt):
        for slot in range(8):
            kt = 4 * j - 2 + slot
            if kt < 0 or kt >= T // 128:
                continue
            for krl in range(2):
                kr = 2 * kt + krl
                for qrl in range(8):
                    qr = 8 * j + qrl
                    if kr // rs != qr // rs:
                        continue
                    r0 = min(max(qr % rs - 4, 0), rs - 8)
                    if r0 <= kr % rs < r0 + 8:
                        rm[krl * 64:(krl + 1) * 64, j, slot, qrl] = 1.0
    cm = np.full((64, 64), NEG, np.float32)
    for qc in range(64):
        c0 = min(max(qc - 8, 0), 48)
        cm[c0:c0 + 16, qc] = 0.0
    cm = np.concatenate([cm, cm], axis=0)
    rel = np.arange(FV) - FV // 2
    bk = _t5_bucket_np(rel)
    oh = np.zeros((32, FV), np.float32)
    oh[bk, np.arange(FV)] = 1.0
    mv = np.zeros((32, FV), np.float32)
    for g, dil in enumerate(DIL):
        ok = (rel % dil == 0) & (np.abs(rel) <= 64 * dil)
        mv[8 + 8 * g: 16 + 8 * g, :] = np.where(ok, 0.0, NEG)[None, :]
    return {
        "rope_c": rope_c, "rope_s": rope_s, "rowmask": rm, "colmask": cm, "oh_t5": oh, "maskvec": mv,
        "crossb": np.full((128, 1), 0.0 if cross else NEG, np.float32),
        "ident": np.eye(128, dtype=np.float32).astype(ml_dtypes.bfloat16),
    }


def cols(v):
    v = np.asarray(v, np.float32)
    return np.ascontiguousarray(v.reshape(-1, 128).T)


def tile_major(x):
    T = x.shape[0]
    return np.ascontiguousarray(x.reshape(T // TT, TT, KC, 128).transpose(0, 3, 2, 1))


def untile_major(y):
    nt = y.shape[0]
    return np.ascontiguousarray(y.transpose(0, 3, 2, 1).reshape(nt * TT, D))


class Prog:
    def __init__(self, n_layers, seg, debug=()):
        self.L = n_layers
        self.seg = seg
        self.T = 2 * seg
        self.NT = self.T // TT
        self.KT = self.T // 128
        self.debug = set(debug)
        nc = bass.Bass("TRN2", target_bir_lowering=False)
        self.nc = nc
        self.s = Sched(nc)
        L, T, NT = self.L, self.T, self.NT

        def inp(name, shape, dt=F32):
            return nc.dram_tensor(name, list(shape), dt, kind="ExternalInput")

        def scr(name, shape, dt):
            if name in self.debug:
                return nc.dram_tensor(name, list(shape), dt, kind="ExternalOutput")
            return nc.dram_tensor(name, list(shape), dt)

        self.xin = inp("xin", [NT, 128, KC, TT])
        self.ln_in_g = inp("ln_in_g", [128, KC])
        self.ln_in_b = inp("ln_in_b", [128, KC])
        self.w_in = inp("w_in", [L, D, D_IN])
        self.na_rpb = inp("na_rpb", [L, 8, 31, 127])
        self.qk_norm_g = inp("qk_norm_g", [L, 2, 64])
        self.diff_lambda = inp("diff_lambda", [L, 128])
        self.diff_subln_g = inp("diff_subln_g", [L, 64])
        self.t5_table = inp("t5_table", [32, 32])
        self.w_gd = inp("w_gate_down", [L, D, 256])
        self.w_gu = inp("w_gate_up", [L, 256, 4 * D])
        self.b_gate = inp("b_gate", [L, 128, 4 * KC])
        self.w_br = inp("w_branch", [L, 2048, D])
        self.w_out = inp("w_out", [L, D, D])
        self.ln1_g = inp("ln1_g", [L, 128, KC])
        self.ln1_b = inp("ln1_b", [L, 128, KC])
        self.w_fi = inp("w_ffn_in", [L, D, 2 * D_FF])
        self.w_fo = inp("w_ffn_out", [L, D_FF, D])
        self.ln2_g = inp("ln2_g", [L, 128, KC])
        self.ln2_b = inp("ln2_b", [L, 128, KC])
        self.rope_c = inp("rope_c", [128, T // 128, 64])
        self.rope_s = inp("rope_s", [128, T // 128, 64])
        self.rowmask = inp("rowmask", [128, NT, 8, 8])
        self.colmask = inp("colmask", [128, 64])
        self.oh_t5 = inp("oh_t5", [32, FV])
        self.maskvec = inp("maskvec", [32, FV])
        self.crossb_in = inp("crossb", [128, 1])
        self.ident_in = inp("ident", [128, 128], BF16)
        self.yout = nc.dram_tensor("yout", [NT, 128, KC, TT], F32, kind="ExternalOutput")
        self.xres = scr("xres", [NT, 128, KC, TT], F32)
        self.xbf = scr("xbf", [NT, 128, KC, TT], BF16)
        self.ypre = scr("ypre", [NT, 128, KC, TT], F32)
        self.qkT = scr("qkT", [QK_ROWS, T], BF16)
        self.vtok = scr("vtok", [T, V_COLS], BF16)
        self.oT = scr("oT", [2048, T], BF16)
        self.mrg = scr("mrg", [NT, 128, KC, TT], BF16)
        self.ffT = scr("ffT", [NT, 128, D_FF // 128, TT], BF16)
        self.fvec = scr("fvec", [32, FV], F32)

    def sb(self, st, name, shape, dt):
        self._uid = getattr(self, "_uid", 0) + 1
        return st.enter_context(self.nc.sbuf_tensor("sb%d_%s" % (self._uid, name), list(shape), dt))

    def build(self):
        nc, s = self.nc, self.s
        with ExitStack() as g:
            self.ps = [g.enter_context(nc.psum_tensor("ps%d" % i, [128, 512], F32)) for i in range(7)]
            self.psb = Ring(self.ps)
            self.pst = g.enter_context(nc.psum_tensor("pst", [128, 1024], BF16))
            self.pstb = Buf("pst")
            self.ident = self.sb(g, "ident", [128, 128], BF16)
            self.ones_f = self.sb(g, "ones_f", [128, 128], F32)
            self.crossb = self.sb(g, "crossb", [128, 1], F32)
            self.zero_c = self.sb(g, "zero_c", [128, 1], F32)
            self.cb = Buf("consts")
            s.dma("sp", self.ident[:], self.ident_in[:, :], self.cb, pwrites=[self.cb])
            s.dma("sp", self.crossb[:], self.crossb_in[:, :], self.cb, pwrites=[self.cb])
            s.op("dve", lambda e: e.memset(self.ones_f[:], 1.0), pwrites=[self.cb])
            s.op("dve", lambda e: e.memset(self.zero_c[:], 0.0), pwrites=[self.cb])
            s.barrier()
            self.phase_ln(self.xin, None, self.ln_in_g, self.ln_in_b, None, last=False)
            self.phase_setup(g)
            for l in range(self.L):
                if "stop_p1" in self.debug:
                    self.phase_p1(l)
                    break
                if "only_attn" in self.debug:
                    self.phase_p1(l)
                    for ph in self.debug_phases:
                        getattr(self, ph)(l)
                    break
                if self.layer(l):
                    break
            s.barrier()
        return nc

    def phase_ln(self, src, l, gsrc, bsrc, dst_unused, last):
        nc, s = self.nc, self.s
        with ExitStack() as st:
            gcol = self.sb(st, "ln_g", [128, KC], F32)
            bcol = self.sb(st, "ln_b", [128, KC], F32)
            cbuf = Buf()
            if l is None:
                s.dma("sp", gcol[:], gsrc[:, :], cbuf, pwrites=[cbuf])
                s.dma("sp", bcol[:], bsrc[:, :], cbuf, pwrites=[cbuf])
            else:
                s.dma("sp", gcol[:], gsrc[l], cbuf, pwrites=[cbuf])
                s.dma("sp", bcol[:], bsrc[l], cbuf, pwrites=[cbuf])
            HT = 256
            yr = Ring([self.sb(st, "ln_y%d" % i, [128, KC, HT], F32) for i in range(2)])
            yqs = [[Buf() for _ in range(4)] for _ in range(2)]
            br = Ring([self.sb(st, "ln_o%d" % i, [128, KC, HT], BF16) for i in range(2)])
            sq = self.sb(st, "ln_sq", [128, 8, HT], F32)
            sqb = Buf()
            acc = self.sb(st, "ln_acc", [128, 5, HT], F32)
            accb = Buf()
            mean = self.sb(st, "ln_mean", [128, HT], F32)
            rstd = self.sb(st, "ln_rstd", [128, HT], F32)
            tmp = self.sb(st, "ln_tmp", [128, HT], F32)
            stb = Buf()
            def lnload(it_):
                yt_, _ = yr.next()
                yq_ = yqs[yr.i]
                s.dma("sp", yt_[:], src[it_ // 2, :, :, (it_ % 2) * HT:(it_ % 2 + 1) * HT], yq_[0], writes=yq_)
                return yt_, yq_

            nxt_ln = lnload(0)
            for it in range(self.NT * 2):
                tt, hh = it // 2, it % 2
                yt, yq = nxt_ln
                if it + 1 < self.NT * 2:
                    nxt_ln = lnload(it + 1)
                ot, ob = br.next()
                s.op("dve", lambda e: e.tensor_reduce(out=acc[:, 0, :], in_=yt[:].rearrange("p k j -> p j k"),
                                                      axis=AX.X, op=ALU.add), reads=yq, pwrites=[accb])
                for q in range(4):
                    s.op("act", lambda e: e.activation(out=sq[:], in_=yt[:, 8 * q:8 * q + 8, :], func=AF.Square),
                         reads=[yq[q]], writes=[sqb])
                    s.op("dve", lambda e: e.tensor_reduce(out=acc[:, 1 + q, :], in_=sq[:].rearrange("p k j -> p j k"),
                                                          axis=AX.X, op=ALU.add), reads=[sqb], pwrites=[accb])
                p1, p1b = self.psb.next()
                s.op("pe", lambda e: e.matmul(p1[:, 0:HT], lhsT=self.ones_f[:], rhs=acc[:, 0, :], start=True, stop=True),
                     reads=[accb, self.cb], writes=[p1b])
                p2, p2b = self.psb.next()
                for q in range(4):
                    s.op("pe", lambda e: e.matmul(p2[:, 0:HT], lhsT=self.ones_f[:], rhs=acc[:, 1 + q, :],
                                                  start=(q == 0), stop=(q == 3)),
                         reads=[accb, self.cb], writes=[p2b] if q == 0 else (), pwrites=() if q == 0 else [p2b],
                         signal=(q == 3))
                s.op("dve", lambda e: e.tensor_scalar(out=mean[:], in0=p1[:, 0:HT], scalar1=1.0 / D, scalar2=None,
                                                      op0=ALU.mult), reads=[p1b], writes=[stb])
                s.op("dve", lambda e: e.tensor_tensor(out=tmp[:], in0=mean[:], in1=mean[:], op=ALU.mult),
                     reads=[stb], pwrites=[stb])
                s.op("dve", lambda e: e.scalar_tensor_tensor(out=tmp[:], in0=p2[:, 0:HT], scalar=1.0 / D, in1=tmp[:],
                                                             op0=ALU.mult, op1=ALU.subtract),
                     reads=[p2b, stb], pwrites=[stb])
                s.op("dve", lambda e: e.tensor_scalar(out=tmp[:], in0=tmp[:], scalar1=LN_EPS, scalar2=None, op0=ALU.add),
                     reads=[stb], pwrites=[stb])
                s.op("act", lambda e: e.activation(out=tmp[:], in_=tmp[:], func=AF.Sqrt), reads=[stb], pwrites=[stb])
                s.op("dve", lambda e: e.reciprocal(out=rstd[:], in_=tmp[:]), reads=[stb], pwrites=[stb])
                mean_b = mean[:].unsqueeze(1).to_broadcast([128, 8, HT])
                rstd_b = rstd[:].unsqueeze(1).to_broadcast([128, 8, HT])
                for q in range(4):
                    sl = slice(8 * q, 8 * q + 8)
                    s.op("dve", lambda e: e.tensor_tensor(out=yt[:, sl, :], in0=yt[:, sl, :], in1=mean_b, op=ALU.subtract),
                         reads=[stb], pwrites=[yq[q]])
                    s.op("dve", lambda e: e.tensor_tensor(out=yt[:, sl, :], in0=yt[:, sl, :], in1=rstd_b, op=ALU.mult),
                         reads=[stb], pwrites=[yq[q]])
                    for k in range(8 * q, 8 * q + 8):
                        s.op("act", lambda e: e.activation(out=yt[:, k, :], in_=yt[:, k, :], func=AF.Identity,
                                                           scale=gcol[:, k:k + 1], bias=bcol[:, k:k + 1]),
                             reads=[cbuf], pwrites=[yq[q]])
                    if not last:
                        s.op("act", lambda e: e.activation(out=ot[:, sl, :], in_=yt[:, sl, :], func=AF.Copy), reads=[yq[q]],
                             writes=[ob] if q == 0 else (), pwrites=() if q == 0 else [ob])
                if last:
                    s.dma("pool", self.yout[tt, :, :, hh * HT:(hh + 1) * HT], yt[:], yq[0], reads=yq)
                else:
                    s.dma("pool", self.xres[tt, :, :, hh * HT:(hh + 1) * HT], yt[:], yq[0], reads=yq)
                    s.dma("pool", self.xbf[tt, :, :, hh * HT:(hh + 1) * HT], ot[:], ob, reads=[ob])
            s.barrier()

    def phase_p1(self, l):
        nc, s = self.nc, self.s
        T, NT = self.T, self.NT
        isq = 0.125
        blocks = [
            (0, 512, "fm", R_QA, isq), (512, 512, "fm", R_KA, 1.0),
            (1536, 512, "fm", R_QB, 32 ** -0.5), (2048, 512, "fm", R_KB, 1.0),
            (3840, 512, "fm", R_QD, isq), (4352, 512, "fm", R_QD + 512, isq), (4864, 512, "fm", R_QD + 1024, isq),
            (5376, 512, "fm", R_KD, 1.0), (5888, 512, "fm", R_KD + 512, 1.0), (6400, 512, "fm", R_KD + 1024, 1.0),
            (1024, 512, "tm", C_VA, 1.0), (2560, 512, "tm", C_VB, 1.0),
            (3072, 512, "qc", 0, 1.0), (3584, 256, "kcvc", 0, 1.0),
            (6912, 512, "tm", C_VD, 1.0), (7424, 512, "tm", C_VD + 512, 1.0), (7936, 512, "tm", C_VD + 1024, 1.0),
        ]
        with ExitStack() as st:
            wr = Ring([self.sb(st, "p1_w%d" % i, [128, KC, 512], BF16) for i in range(2)])
            xr = Ring([self.sb(st, "p1_x%d" % i, [128, KC, TT], BF16) for i in range(2)])
            sr = Ring([self.sb(st, "p1_s%d" % i, [128, 512], BF16) for i in range(4)])
            gq = self.sb(st, "p1_gq", [128, 512], F32)
            gk = self.sb(st, "p1_gk", [128, 128], F32)
            rc = self.sb(st, "p1_rc", [128, T // 128, 64], F32)
            rs_ = self.sb(st, "p1_rs", [128, T // 128, 64], F32)
            cbuf = Buf()
            s.dma("sp", rc[:], self.rope_c[:, :, :], cbuf, pwrites=[cbuf])
            s.dma("sp", rs_[:], self.rope_s[:, :, :], cbuf, pwrites=[cbuf])
            s.dma("sp", gq[:].rearrange("p (h e) -> p h e", h=8),
                  bass.AP(self.qk_norm_g, l * 128, [[0, 128], [0, 8], [1, 64]]), cbuf, pwrites=[cbuf])
            s.dma("sp", gk[:].rearrange("p (h e) -> p h e", h=2),
                  bass.AP(self.qk_norm_g, l * 128 + 64, [[0, 128], [0, 2], [1, 64]]), cbuf, pwrites=[cbuf])
            s.op("dve", lambda e: e.tensor_scalar(out=gq[:], in0=gq[:], scalar1=0.125, scalar2=None, op0=ALU.mult),
                 reads=[cbuf], pwrites=[cbuf])
            f1 = self.sb(st, "p1_f1", [128, 512], F32)
            f2 = self.sb(st, "p1_f2", [128, 512], F32)
            f3 = self.sb(st, "p1_f3", [128, 512], F32)
            sm = self.sb(st, "p1_sm", [128, 3, 8], F32)
            fb = Buf()
            qst = self.sb(st, "p1_qst", [128, 4, TT], BF16)
            qsb = Buf()
            kst = self.sb(st, "p1_kst", [128, TT], BF16)
            ksb = Buf()
            tmb = self.sb(st, "p1_tmb", [128, 512], BF16)
            tmbb = Buf()

            def wload(bi):
                c0, ncol = blocks[bi][0], blocks[bi][1]
                wt, wb = wr.next()
                src = bass.AP(self.w_in, l * D * D_IN + c0, [[D_IN, 128], [128 * D_IN, KC], [1, ncol]])
                s.dma("pool", wt[:, :, 0:ncol], src, wb, writes=[wb])
                return wt, wb

            def xload(tt):
                xt, xb = xr.next()
                s.dma("sp", xt[:], self.xbf[tt], xb, writes=[xb])
                return xt, xb

            def normrope(ps_ap, psbuf, width, gt, tsub, out_bf, outbuf):
                nh = width // 64
                s.op("act", lambda e: e.activation(out=f1[:, 0:width], in_=ps_ap, func=AF.Square),
                     reads=[psbuf], writes=[fb])
                s.op("dve", lambda e: e.tensor_reduce(out=sm[:, 0, 0:nh], in_=f1[:, 0:width].rearrange("p (h e) -> p h e", e=64),
                                                      axis=AX.X, op=ALU.add), reads=[fb], pwrites=[fb])
                s.op("dve", lambda e: e.tensor_scalar(out=sm[:, 1, 0:nh], in0=sm[:, 0, 0:nh], scalar1=1.0 / 64, scalar2=RMS_EPS,
                                                      op0=ALU.mult, op1=ALU.add), reads=[fb], pwrites=[fb])
                s.op("act", lambda e: e.activation(out=sm[:, 1, 0:nh], in_=sm[:, 1, 0:nh], func=AF.Sqrt), reads=[fb], pwrites=[fb])
                s.op("dve", lambda e: e.reciprocal(out=sm[:, 2, 0:nh], in_=sm[:, 1, 0:nh]), reads=[fb], pwrites=[fb])
                s.op("dve", lambda e: e.tensor_tensor(out=f1[:, 0:width].rearrange("p (h e) -> p h e", e=64),
                                                      in0=ps_ap.rearrange("p (h e) -> p h e", e=64),
                                                      in1=sm[:, 2, 0:nh].rearrange("p (h o) -> p h o", o=1).to_broadcast([128, nh, 64]),
                                                      op=ALU.mult), reads=[psbuf, fb], pwrites=[fb])
                s.op("pool", lambda e: e.tensor_tensor(out=f1[:, 0:width], in0=f1[:, 0:width], in1=gt, op=ALU.mult),
                     reads=[fb, cbuf], pwrites=[fb])
                cb_ = rc[:, tsub, :].rearrange("p (o e) -> p o e", o=1).to_broadcast([128, nh, 64])
                sb_ = rs_[:, tsub, :].rearrange("p (o e) -> p o e", o=1).to_broadcast([128, nh, 64])
                s.op("dve", lambda e: e.tensor_tensor(out=f2[:, 0:width].rearrange("p (h e) -> p h e", e=64),
                                                      in0=f1[:, 0:width].rearrange("p (h e) -> p h e", e=64), in1=cb_, op=ALU.mult),
                     reads=[fb, cbuf], pwrites=[fb])
                sw = f1[:, 0:width].rearrange("p (h a two) -> p h a two", two=2, a=32)[:, :, :, ::-1]
                sb4 = rs_[:, tsub, :].rearrange("p (a two) -> p a two", two=2).unsqueeze(1).to_broadcast([128, nh, 32, 2])
                s.op("pool", lambda e: e.tensor_tensor(out=f3[:, 0:width].rearrange("p (h a two) -> p h a two", two=2, a=32), in0=sw,
                                                       in1=sb4, op=ALU.mult),
                     reads=[fb, cbuf], pwrites=[fb])
                s.op("dve", lambda e: e.tensor_tensor(out=out_bf, in0=f2[:, 0:width], in1=f3[:, 0:width], op=ALU.add),
                     reads=[fb], writes=[outbuf])

            ev = 0
            nxt = wload(0)
            for bi, (c0, ncol, kind, dst, scale) in enumerate(blocks):
                wt, wb = nxt
                if bi + 1 < len(blocks):
                    nxt = wload(bi + 1)
                xn = xload(0)
                for tt in range(NT):
                    xt, xb = xn
                    if tt + 1 < NT:
                        xn = xload(tt + 1)
                    for gi in range(4):
                        pt, pb = self.psb.next()
                        if kind == "fm":
                            for k in range(KC):
                                s.op("pe", lambda e: e.matmul(pt[:], lhsT=wt[:, k, gi * 128:(gi + 1) * 128], rhs=xt[:, k, :],
                                                              start=(k == 0), stop=(k == KC - 1)),
                                     reads=[wb, xb], writes=[pb] if k == 0 else (), pwrites=() if k == 0 else [pb],
                                     signal=(k == KC - 1))
                            stt, stb_ = sr.next()
                            if ev % 2 == 0:
                                s.op("act", lambda e: e.activation(out=stt[:], in_=pt[:], func=AF.Copy, scale=float(scale)),
                                     reads=[pb], writes=[stb_])
                            else:
                                s.op("dve", lambda e: e.tensor_scalar(out=stt[:], in0=pt[:], scalar1=float(scale), scalar2=None,
                                                                      op0=ALU.mult), reads=[pb], writes=[stb_])
                            ev += 1
                            s.dma("sp", self.qkT[dst + gi * 128: dst + (gi + 1) * 128, tt * TT:(tt + 1) * TT], stt[:], stb_,
                                  reads=[stb_])
                        else:
                            for k in range(KC):
                                s.op("pe", lambda e: e.matmul(pt[:, 0:ncol], lhsT=xt[:, k, gi * 128:(gi + 1) * 128], rhs=wt[:, k, 0:ncol],
                                                              start=(k == 0), stop=(k == KC - 1)),
                                     reads=[wb, xb], writes=[pb] if k == 0 else (), pwrites=() if k == 0 else [pb],
                                     signal=(k == KC - 1))
                            tsub = tt * 4 + gi
                            r0 = tt * TT + gi * 128
                            if kind == "tm":
                                stt, stb_ = sr.next()
                                if ev % 2 == 0:
                                    s.op("act", lambda e: e.activation(out=stt[:], in_=pt[:], func=AF.Copy), reads=[pb], writes=[stb_])
                                else:
                                    s.op("dve", lambda e: e.tensor_copy(out=stt[:], in_=pt[:]), reads=[pb], writes=[stb_])
                                ev += 1
                                s.dma("sp", self.vtok[r0:r0 + 128, dst:dst + 512], stt[:], stb_, reads=[stb_])
                            elif kind == "qc":
                                normrope(pt[:], pb, 512, gq[:], tsub, tmb[:], tmbb)
                                for c in range(4):
                                    s.op("pe", lambda e: e.transpose(out=self.pst[:, c * 128:(c + 1) * 128], in_=tmb[:, c * 128:(c + 1) * 128],
                                                                     identity=self.ident[:]),
                                         reads=[tmbb, self.cb], writes=[self.pstb] if c == 0 else (),
                                         pwrites=() if c == 0 else [self.pstb], signal=(c == 3))
                                s.op("act", lambda e: e.activation(out=qst[:, :, gi * 128:(gi + 1) * 128],
                                                                   in_=self.pst[:, 0:512].rearrange("p (c j) -> p c j", c=4), func=AF.Copy),
                                     reads=[self.pstb], pwrites=[qsb])
                                if gi == 3:
                                    dsta = bass.AP(self.qkT, R_QC * T + tt * TT, [[T, 128], [128 * T, 4], [1, TT]])
                                    s.dma("sp", dsta, qst[:], qsb, reads=[qsb])
                            else:
                                stt, stb_ = sr.next()
                                s.op("act", lambda e: e.activation(out=stt[:, 0:128], in_=pt[:, 128:256], func=AF.Copy),
                                     reads=[pb], writes=[stb_])
                                s.dma("sp", self.vtok[r0:r0 + 128, C_VC:C_VC + 128], stt[:, 0:128], stb_, reads=[stb_])
                                normrope(pt[:, 0:128], pb, 128, gk[:], tsub, tmb[:, 0:128], tmbb)
                                s.op("pe", lambda e: e.transpose(out=self.pst[:, 512:640], in_=tmb[:, 0:128], identity=self.ident[:]),
                                     reads=[tmbb, self.cb], writes=[self.pstb])
                                s.op("act", lambda e: e.activation(out=kst[:, gi * 128:(gi + 1) * 128], in_=self.pst[:, 512:640], func=AF.Copy),
                                     reads=[self.pstb], pwrites=[ksb])
                                if gi == 3:
                                    s.dma("sp", self.qkT[R_KC:R_KC + 128, tt * TT:(tt + 1) * TT], kst[:], ksb, reads=[ksb])
            s.barrier()

    def phase_setup(self, g):
        nc, s = self.nc, self.s
        self.bfar = self.sb(g, "bfar", [128, 8, 4], F32)
        self.ones_b = self.sb(g, "ones_b", [128, 64], BF16)
        with ExitStack() as st:
            t5 = self.sb(st, "su_t5", [32, 32], F32)
            oh = self.sb(st, "su_oh", [32, FV], F32)
            mv = self.sb(st, "su_mv", [32, FV], F32)
            fv = self.sb(st, "su_fv", [32, FV], F32)
            b = Buf()
            s.dma("sp", t5[:], self.t5_table[:, :], b, pwrites=[b])
            s.dma("sp", oh[:], self.oh_t5[:, :], b, pwrites=[b])
            s.dma("sp", mv[:], self.maskvec[:, :], b, pwrites=[b])
            for h in range(8):
                for j, bk in enumerate((15, 31)):
                    s.dma("sp", self.bfar[:, h, j:j + 1], bass.AP(self.t5_table, bk * 32 + h, [[0, 128], [1, 1]]), b, pwrites=[b])
            s.op("dve", lambda e: e.tensor_tensor(out=self.bfar[:, :, 2:4], in0=self.bfar[:, :, 0:2],
                                                  in1=self.crossb[:].unsqueeze(1).to_broadcast([128, 8, 2]), op=ALU.add),
                 reads=[self.cb], pwrites=[b])
            s.op("dve", lambda e: e.memset(self.ones_b[:], 1.0), pwrites=[b])
            fb = Buf()
            for c in range(FV // 512):
                pt, pb = self.psb.next()
                s.op("pe", lambda e: e.matmul(pt[0:32, :], lhsT=t5[:], rhs=oh[:, c * 512:(c + 1) * 512], start=True, stop=True),
                     reads=[b], writes=[pb])
                s.op("dve", lambda e: e.tensor_tensor(out=fv[:, c * 512:(c + 1) * 512], in0=pt[0:32, :], in1=mv[:, c * 512:(c + 1) * 512],
                                                      op=ALU.add), reads=[pb, b], pwrites=[fb])
            s.dma("sp", self.fvec[:, :], fv[:], fb, reads=[fb])
            s.barrier()

    def attn_begin(self, st, n_obanks, tag):
        a = {}
        a["obanks"] = [(self.ps[i], Buf()) for i in range(n_obanks)]
        a["sring"] = Ring(self.ps[n_obanks:7])
        a["pt"] = Ring([self.sb(st, tag + "_pt%d" % i, [128, 512], BF16) for i in range(5)])
        a["rr"] = Ring([self.sb(st, tag + "_rr%d" % i, [128, 512], F32) for i in range(2)])
        a["stg"] = Ring([self.sb(st, tag + "_st%d" % i, [128, 512], BF16) for i in range(2)])
        a["pend"] = []
        return a

    def attn_tile(self, a, k_ap, q_ap, kq_bufs, bias_ap, bias_bufs, colbias, mask_ap, mask_bufs, v_ap, v_bufs,
                  obank, first, last, kdim):
        s = self.s
        sp_, sb_ = a["sring"].next()
        pt, ptb = a["pt"].next()
        if bias_ap is not None:
            s.op("pe", lambda e: e.matmul(sp_[:], lhsT=k_ap, rhs=q_ap, start=True, stop=False), reads=kq_bufs, writes=[sb_],
                 signal=False)
            s.op("pe", lambda e: e.matmul(sp_[:], lhsT=self.ident[:], rhs=bias_ap, start=False, stop=True),
                 reads=list(bias_bufs) + [self.cb], pwrites=[sb_])
        else:
            s.op("pe", lambda e: e.matmul(sp_[:], lhsT=k_ap, rhs=q_ap, start=True, stop=True), reads=kq_bufs, writes=[sb_])
        s.op("act", lambda e: e.activation(out=pt[:], in_=sp_[:], func=AF.Exp, bias=colbias, scale=1.0),
             reads=[sb_, self.cb], writes=[ptb])
        if mask_ap is not None:
            s.op("pool", lambda e: e.tensor_tensor(out=pt[:].rearrange("p (a b) -> p a b", b=64),
                                                   in0=pt[:].rearrange("p (a b) -> p a b", b=64), in1=mask_ap, op=ALU.mult),
                 reads=list(mask_bufs), pwrites=[ptb])
        a["pend"].append((pt, ptb, v_ap, v_bufs, obank, first, last))
        if len(a["pend"]) > 2:
            self.attn_pv(a)

    def attn_pv(self, a):
        s = self.s
        pt, ptb, v_ap, v_bufs, (ob, obb), first, last = a["pend"].pop(0)
        s.op("pe", lambda e: e.matmul(ob[:], lhsT=v_ap, rhs=pt[:], start=first, stop=last),
             reads=[ptb] + list(v_bufs), writes=[obb] if first else (), pwrites=() if first else [obb], signal=last)

    def attn_flush(self, a):
        while a["pend"]:
            self.attn_pv(a)

    def attn_finish_simple(self, a, obank, row0, qt):
        s = self.s
        ob, obb = obank
        rr, rrb = a["rr"].next()
        s.op("dve", lambda e: e.reciprocal(out=rr[64:128, :], in_=ob[64:128, :]), reads=[obb], writes=[rrb])
        sg, sgb = a["stg"].next()
        s.op("dve", lambda e: e.tensor_tensor(out=sg[0:64, :], in0=ob[0:64, :], in1=rr[64:128, :], op=ALU.mult),
             reads=[obb, rrb], writes=[sgb])
        s.dma("sp", self.oT[row0:row0 + 64, qt * TT:(qt + 1) * TT], sg[0:64, :], sgb, reads=[sgb])

    def vaug_ring(self, st, tag, n, ng=1):
        tiles = [self.sb(st, tag + "_va%d" % i, [128, self.KT, ng, 128], BF16) for i in range(n)]
        r = Ring(tiles)
        for t, b in zip(r.tiles, r.bufs):
            self.s.op("pool", lambda e: e.memset(t[:, :, :, 64:128], 1.0), pwrites=[b])
        return r

    def vaug_fill(self, t, b, col0s):
        for g, c0 in enumerate(col0s):
            self.s.dma("sp", t[:, :, g, 0:64], bass.AP(self.vtok, c0, [[V_COLS, 128], [128 * V_COLS, self.KT], [1, 64]]), b,
                       writes=[b] if g == 0 else (), pwrites=() if g == 0 else [b])

    def hankel(self, dst_ap, vec_handle, offset, home, pw):
        n = dst_ap.shape[-1]
        self.s.dma("sp", dst_ap, bass.AP(vec_handle, offset, [[1, dst_ap.shape[0]], [1, n]]), home, pwrites=pw)

    def phase_pc(self, l):
        nc, s = self.nc, self.s
        T, NT, KT = self.T, self.NT, self.KT
        with ExitStack() as st:
            a = self.attn_begin(st, 2, "pc")
            oi = 0
            kc = [self.sb(st, "pc_k%d" % i, [128, T], BF16) for i in range(2)]
            qb = Buf()
            for kv in range(2):
                s.op("pool", lambda e: e.memset(kc[kv][:], 0.0), pwrites=[qb])
                s.dma("sp", kc[kv][0:64, :], self.qkT[R_KC + kv * 64:R_KC + (kv + 1) * 64, :], qb, pwrites=[qb])
            qr = Ring([self.sb(st, "pc_q%d" % i, [128, T], BF16) for i in range(2)])
            for t_, b_ in zip(qr.tiles, qr.bufs):
                s.op("pool", lambda e: e.memset(t_[:], 0.0), pwrites=[b_])
            vr = self.vaug_ring(st, "pc", 2)
            for h in range(8):
                kv = h // 4
                if h % 4 == 0:
                    va, vb = vr.next()
                    self.vaug_fill(va, vb, [C_VC + kv * 64])
                qh, qhb = qr.next()
                s.dma("sp", qh[0:64, :], self.qkT[R_QC + h * 64:R_QC + (h + 1) * 64, :], qhb, pwrites=[qhb])
                for qt in range(NT):
                    for kt in range(KT):
                        cross = (kt * 128) // self.seg != (qt * TT) // self.seg
                        self.attn_tile(a, kc[kv][:, kt * 128:(kt + 1) * 128],
                                       qh[:, qt * TT:(qt + 1) * TT], [qb, qhb],
                                       None, (), self.crossb[:] if cross else self.zero_c[:], None, (),
                                       va[:, kt, 0, :], [vb], a["obanks"][oi], kt == 0, kt == KT - 1, 64)
                    self.attn_flush(a)
                    self.attn_finish_simple(a, a["obanks"][oi], 1024 + h * 64, qt)
                    oi ^= 1
            s.barrier()

    def phase_pa(self, l):
        nc, s = self.nc, self.s
        T, NT, KT = self.T, self.NT, self.KT
        with ExitStack() as st:
            a = self.attn_begin(st, 2, "pa")
            oi = 0
            qr = Ring([self.sb(st, "pa_q%d" % i, [128, T], BF16) for i in range(2)])
            kr = Ring([self.sb(st, "pa_k%d" % i, [128, T], BF16) for i in range(2)])
            for r_ in (qr, kr):
                for t_, b_ in zip(r_.tiles, r_.bufs):
                    s.op("pool", lambda e: e.memset(t_[:], 0.0), pwrites=[b_])
            vr = self.vaug_ring(st, "pa", 2)
            rm = self.sb(st, "pa_rm", [128, NT, 8, 8], F32)
            cm = self.sb(st, "pa_cm", [128, 64], F32)
            cb = Buf()
            s.dma("sp", rm[:], self.rowmask[:, :, :, :], cb, pwrites=[cb])
            s.dma("sp", cm[:], self.colmask[:, :], cb, pwrites=[cb])
            rmb = self.sb(st, "pa_rmb", [128, NT, 8, 8], BF16)
            s.op("dve", lambda e: e.tensor_copy(out=rmb[:], in_=rm[:]), reads=[cb], pwrites=[cb])
            hs = Ring([self.sb(st, "pa_hs%d" % i, [128, 512], F32) for i in range(4)])
            bt = Ring([self.sb(st, "pa_bt%d" % i, [128, 8, 512], BF16) for i in range(2)])
            for h in range(8):
                va, vb = vr.next()
                self.vaug_fill(va, vb, [C_VA + h * 64])
                bias, bb = bt.next()
                for slot in range(8):
                    ht, hb = hs.next()
                    for krl in range(2):
                        off = ((l * 8 + h) * 31 + (2 * slot + 4 + krl)) * 127
                        s.dma("sp", ht[krl * 64:(krl + 1) * 64, :].rearrange("p (a b) -> p a b", b=64),
                              bass.AP(self.na_rpb, off, [[1, 64], [127, 8], [1, 64]]), hb,
                              writes=[hb] if krl == 0 else (), pwrites=() if krl == 0 else [hb])
                    s.op("dve", lambda e: e.tensor_tensor(out=bias[:, slot, :].rearrange("p (a b) -> p a b", b=64),
                                                          in0=ht[:, ::-1].rearrange("p (a b) -> p a b", b=64),
                                                          in1=cm[:].unsqueeze(1).to_broadcast([128, 8, 64]), op=ALU.add),
                         reads=[hb, cb], writes=[bb] if slot == 0 else (), pwrites=() if slot == 0 else [bb])
                qh, qhb = qr.next()
                kh, khb = kr.next()
                s.dma("sp", qh[0:64, :], self.qkT[R_QA + h * 64:R_QA + (h + 1) * 64, :], qhb, pwrites=[qhb])
                s.dma("sp", kh[0:64, :], self.qkT[R_KA + h * 64:R_KA + (h + 1) * 64, :], khb, pwrites=[khb])
                for qt in range(NT):
                    kts = [kt for kt in range(4 * qt - 2, 4 * qt + 6) if 0 <= kt < KT]
                    for i, kt in enumerate(kts):
                        slot = kt - 4 * qt + 2
                        self.attn_tile(a, kh[:, kt * 128:(kt + 1) * 128], qh[:, qt * TT:(qt + 1) * TT], [qhb, khb],
                                       bias[:, slot, :], [bb], self.zero_c[:],
                                       rmb[:, qt, slot, :].unsqueeze(2).to_broadcast([128, 8, 64]), [cb],
                                       va[:, kt, 0, :], [vb], a["obanks"][oi], i == 0, i == len(kts) - 1, 64)
                    self.attn_flush(a)
                    self.attn_finish_simple(a, a["obanks"][oi], 0 + h * 64, qt)
                    oi ^= 1
            s.barrier()

    def phase_pd(self, l):
        nc, s = self.nc, self.s
        T, NT, KT = self.T, self.NT, self.KT
        with ExitStack() as st:
            a = self.attn_begin(st, 2, "pd")
            oi = 0
            qd = self.sb(st, "pd_q", [128, 3, T], BF16)
            kd = self.sb(st, "pd_k", [128, 3, T], BF16)
            qb = Buf()
            s.op("pool", lambda e: e.memset(qd[:], 0.0), pwrites=[qb])
            s.op("pool", lambda e: e.memset(kd[:], 0.0), pwrites=[qb])
            vr = self.vaug_ring(st, "pd", 1, 3)
            va, vb = vr.next()
            ranges = [(-1, 4), (-2, 5), (-8, 11)]
            ntile = sum(hi - lo + 1 for lo, hi in ranges)
            bias = self.sb(st, "pd_bias", [128, ntile, 512], BF16)
            bb = Buf()
            hs = Ring([self.sb(st, "pd_hs%d" % i, [128, 512], F32) for i in range(4)])
            for sl in range(8):
                for g in range(3):
                    r0 = R_QD + (g * 8 + sl) * 64
                    s.dma("sp", qd[0:64, g, :], self.qkT[r0:r0 + 64, :], qb, pwrites=[qb])
                    r0 = R_KD + (g * 8 + sl) * 64
                    s.dma("sp", kd[0:64, g, :], self.qkT[r0:r0 + 64, :], qb, pwrites=[qb])
                self.vaug_fill(va, vb, [C_VD + (g * 8 + sl) * 64 for g in range(3)])
                ti = 0
                tidx = {}
                for g in range(3):
                    lo, hi = ranges[g]
                    for o in range(lo, hi + 1):
                        ht, hb = hs.next()
                        s.dma("sp", ht[:], bass.AP(self.fvec, (8 + 8 * g + sl) * FV + 1025 + o * 128, [[1, 128], [1, 512]]), hb, writes=[hb])
                        s.op("dve", lambda e: e.tensor_copy(out=bias[:, ti, :], in_=ht[:, ::-1]),
                             reads=[hb], writes=[bb] if ti == 0 else (), pwrites=() if ti == 0 else [bb])
                        tidx[(g, o)] = ti
                        ti += 1
                for qt in range(NT):
                    tl = []
                    for g in range(3):
                        lo, hi = ranges[g]
                        for o in range(lo, hi + 1):
                            kt = 4 * qt + o
                            if 0 <= kt < KT:
                                tl.append((g, o, kt))
                    for i, (g, o, kt) in enumerate(tl):
                        cross = (kt * 128) // self.seg != (qt * TT) // self.seg
                        self.attn_tile(a, kd[:, g, kt * 128:(kt + 1) * 128], qd[:, g, qt * TT:(qt + 1) * TT], [qb],
                                       bias[:, tidx[(g, o)], :], [bb], self.crossb[:] if cross else self.zero_c[:], None, (),
                                       va[:, kt, g, :], [vb], a["obanks"][oi], i == 0, i == len(tl) - 1, 64)
                    self.attn_flush(a)
                    self.attn_finish_simple(a, a["obanks"][oi], 1536 + sl * 64, qt)
                    oi ^= 1
            s.barrier()

    def phase_pb(self, l):
        nc, s = self.nc, self.s
        T, NT, KT = self.T, self.NT, self.KT
        lam_init = 0.8 - 0.6 * math.exp(-0.3 * l)
        with ExitStack() as st:
            a = self.attn_begin(st, 4, "pb")
            oi = 0
            lq = self.sb(st, "pb_lq", [128, 128], F32)
            lt = self.sb(st, "pb_lt", [128, 8], F32)
            gcol = self.sb(st, "pb_g", [64, 1], F32)
            lb = Buf()
            s.dma("sp", lq[:], bass.AP(self.diff_lambda, l * 128, [[0, 128], [1, 128]]), lb, pwrites=[lb])
            s.dma("sp", gcol[:], bass.AP(self.diff_subln_g, l * 64, [[1, 64], [1, 1]]), lb, pwrites=[lb])
            s.op("dve", lambda e: e.tensor_tensor(out=lq[:, 0:32], in0=lq[:, 0:32], in1=lq[:, 32:64], op=ALU.mult), reads=[lb], pwrites=[lb])
            s.op("dve", lambda e: e.tensor_tensor(out=lq[:, 64:96], in0=lq[:, 64:96], in1=lq[:, 96:128], op=ALU.mult), reads=[lb], pwrites=[lb])
            s.op("dve", lambda e: e.tensor_reduce(out=lt[:, 0:1], in_=lq[:, 0:32], axis=AX.X, op=ALU.add), reads=[lb], pwrites=[lb])
            s.op("dve", lambda e: e.tensor_reduce(out=lt[:, 1:2], in_=lq[:, 64:96], axis=AX.X, op=ALU.add), reads=[lb], pwrites=[lb])
            s.op("act", lambda e: e.activation(out=lt[:, 2:4], in_=lt[:, 0:2], func=AF.Exp), reads=[lb], pwrites=[lb])
            s.op("dve", lambda e: e.tensor_tensor(out=lt[:, 4:5], in0=lt[:, 3:4], in1=lt[:, 2:3], op=ALU.subtract), reads=[lb], pwrites=[lb])
            s.op("dve", lambda e: e.tensor_scalar(out=lt[:, 5:6], in0=lt[:, 4:5], scalar1=-lam_init, scalar2=None, op0=ALU.add), reads=[lb], pwrites=[lb])
            s.op("dve", lambda e: e.tensor_scalar(out=gcol[:], in0=gcol[:], scalar1=1.0 - lam_init, scalar2=None, op0=ALU.mult), reads=[lb], pwrites=[lb])
            nlam = lt[0:64, 5:6]
            vr = self.vaug_ring(st, "pb", 2)
            qr = Ring([self.sb(st, "pb_q%d" % i, [128, 2, T], BF16) for i in range(2)])
            kr = Ring([self.sb(st, "pb_k%d" % i, [128, 2, T], BF16) for i in range(2)])
            for r_ in (qr, kr):
                for t_, b_ in zip(r_.tiles, r_.bufs):
                    s.op("pool", lambda e: e.memset(t_[:], 0.0), pwrites=[b_])
            hs = Ring([self.sb(st, "pb_hs%d" % i, [128, 512], F32) for i in range(4)])
            bt = Ring([self.sb(st, "pb_bt%d" % i, [128, 6, 512], BF16) for i in range(2)])
            rr = self.sb(st, "pb_rr", [128, 2, 512], F32)
            od = self.sb(st, "pb_od", [64, 3, 512], F32)
            wb_ = Buf()
            for h in range(8):
                qh, qhb = qr.next()
                kh, khb = kr.next()
                va, vb = vr.next()
                self.vaug_fill(va, vb, [C_VB + h * 64])
                for c in range(2):
                    r0 = R_QB + h * 64 + c * 32
                    s.dma("sp", qh[0:32, c, :], self.qkT[r0:r0 + 32, :], qhb, pwrites=[qhb])
                    r0 = R_KB + h * 64 + c * 32
                    s.dma("sp", kh[0:32, c, :], self.qkT[r0:r0 + 32, :], khb, pwrites=[khb])
                bias, bb = bt.next()
                for bi_, o in enumerate(range(-1, 5)):
                    ht, hb = hs.next()
                    s.dma("sp", ht[:], bass.AP(self.fvec, h * FV + 1025 + o * 128, [[1, 128], [1, 512]]), hb, writes=[hb])
                    s.op("dve", lambda e: e.tensor_copy(out=bias[:, bi_, :], in_=ht[:, ::-1]),
                         reads=[hb], writes=[bb] if bi_ == 0 else (), pwrites=() if bi_ == 0 else [bb])
                for qt in range(NT):
                    for c in range(2):
                        for kt in range(KT):
                            o = kt - 4 * qt
                            cross = (kt * 128) // self.seg != (qt * TT) // self.seg
                            near = -1 <= o <= 4
                            if near:
                                colb = self.crossb[:] if cross else self.zero_c[:]
                                bap, bbufs = bias[:, o + 1, :], [bb]
                            else:
                                j = (0 if o < 0 else 1) + (2 if cross else 0)
                                colb = self.bfar[:, h, j:j + 1]
                                bap, bbufs = None, ()
                            self.attn_tile(a, kh[:, c, kt * 128:(kt + 1) * 128], qh[:, c, qt * TT:(qt + 1) * TT], [qhb, khb],
                                           bap, bbufs, colb, None, (), va[:, kt, 0, :], [vb], a["obanks"][oi + c],
                                           kt == 0, kt == KT - 1, 32)
                        self.attn_flush(a)
                    for c in range(2):
                        ob, obb = a["obanks"][oi + c]
                        s.op("dve", lambda e: e.reciprocal(out=rr[64:128, c, :], in_=ob[64:128, :]), reads=[obb], pwrites=[wb_])
                        s.op("dve", lambda e: e.tensor_tensor(out=od[:, c, :], in0=ob[0:64, :], in1=rr[64:128, c, :], op=ALU.mult),
                             reads=[obb, wb_], pwrites=[wb_])
                    oi ^= 2
                    s.op("dve", lambda e: e.scalar_tensor_tensor(out=od[:, 2, :], in0=od[:, 1, :], scalar=nlam, in1=od[:, 0, :],
                                                                 op0=ALU.mult, op1=ALU.add), reads=[wb_, lb], pwrites=[wb_])
                    s.op("pool", lambda e: e.tensor_tensor(out=od[:, 0, :], in0=od[:, 2, :], in1=od[:, 2, :], op=ALU.mult), reads=[wb_], pwrites=[wb_])
                    mp, mpb = a["sring"].next()
                    s.op("pe", lambda e: e.matmul(mp[0:64, :], lhsT=self.ones_f[0:64, 0:64], rhs=od[:, 0, :], start=True, stop=True),
                         reads=[wb_, self.cb], writes=[mpb])
                    s.op("dve", lambda e: e.tensor_scalar(out=od[:, 1, :], in0=mp[0:64, :], scalar1=1.0 / 64, scalar2=RMS_EPS,
                                                          op0=ALU.mult, op1=ALU.add), reads=[mpb], pwrites=[wb_])
                    s.op("act", lambda e: e.activation(out=od[:, 1, :], in_=od[:, 1, :], func=AF.Sqrt), reads=[wb_], pwrites=[wb_])
                    s.op("dve", lambda e: e.reciprocal(out=od[:, 0, :], in_=od[:, 1, :]), reads=[wb_], pwrites=[wb_])
                    s.op("dve", lambda e: e.tensor_tensor(out=od[:, 1, :], in0=od[:, 2, :], in1=od[:, 0, :], op=ALU.mult), reads=[wb_], pwrites=[wb_])
                    sg, sgb = a["stg"].next()
                    s.op("dve", lambda e: e.tensor_scalar(out=sg[0:64, :], in0=od[:, 1, :], scalar1=gcol[:, 0:1], scalar2=None, op0=ALU.mult),
                         reads=[wb_, lb], writes=[sgb])
                    s.dma("sp", self.oT[512 + h * 64:512 + (h + 1) * 64, qt * TT:(qt + 1) * TT], sg[0:64, :], sgb, reads=[sgb])
            s.barrier()

    def phase_p2a(self, l):
        nc, s = self.nc, self.s
        T, NT = self.T, self.NT
        with ExitStack() as st0:
            gz = self.sb(st0, "p2_gz", [128, 2, T], BF16)
            gzb = Buf()
            with ExitStack() as st:
                wgd = self.sb(st, "p2_wgd", [128, KC, 256], BF16)
                wb = Buf()
                s.dma("pool", wgd[:], bass.AP(self.w_gd, l * D * 256, [[256, 128], [128 * 256, KC], [1, 256]]), wb, writes=[wb])
                xr = Ring([self.sb(st, "p2_x%d" % i, [128, KC, TT], BF16) for i in range(2)])
                for tt in range(NT):
                    xt, xb = xr.next()
                    s.dma("sp", xt[:], self.xbf[tt], xb, writes=[xb])
                    for c in range(2):
                        pt, pb = self.psb.next()
                        for k in range(KC):
                            s.op("pe", lambda e: e.matmul(pt[:], lhsT=wgd[:, k, c * 128:(c + 1) * 128], rhs=xt[:, k, :], start=(k == 0), stop=(k == KC - 1)),
                                 reads=[wb, xb], writes=[pb] if k == 0 else (), pwrites=() if k == 0 else [pb], signal=(k == KC - 1))
                        s.op("act", lambda e: e.activation(out=gz[:, c, tt * TT:(tt + 1) * TT], in_=pt[:], func=AF.Copy), reads=[pb], pwrites=[gzb])
                s.barrier()
            with ExitStack() as st:
                wgr = Ring([self.sb(st, "p2_wgu%d" % i, [128, 2, 4, 512], BF16) for i in range(2)])
                wbr = Ring([self.sb(st, "p2_wbr%d" % i, [128, 16, 512], BF16) for i in range(2)])
                orr = Ring([self.sb(st, "p2_o%d" % i, [128, 16, TT], BF16) for i in range(2)])
                bg = self.sb(st, "p2_bg", [128, 4 * KC], F32)
                bgb = Buf()
                s.dma("sp", bg[:], self.b_gate[l], bgb, writes=[bgb])
                gts = Ring([self.sb(st, "p2_gt%d" % i, [128, 512], F32) for i in range(4)])
                tms = Ring([self.sb(st, "p2_tm%d" % i, [128, 512], F32) for i in range(4)])
                m1 = Ring([self.sb(st, "p2_m%d" % i, [128, 512], F32) for i in range(2)])
                sr = Ring([self.sb(st, "p2_s%d" % i, [128, 512], BF16) for i in range(3)])

                def wload(db):
                    w1, b1 = wgr.next()
                    s.dma("pool", w1[:, 0, :, :], bass.AP(self.w_gu, l * 256 * 4 * D + db * 512, [[4 * D, 128], [D, 4], [1, 512]]), b1, writes=[b1])
                    s.dma_more("pool", w1[:, 1, :, :], bass.AP(self.w_gu, l * 256 * 4 * D + 128 * 4 * D + db * 512, [[4 * D, 128], [D, 4], [1, 512]]), b1, b1)
                    w2, b2 = wbr.next()
                    s.dma("pool", w2[:], bass.AP(self.w_br, l * 2048 * D + db * 512, [[D, 128], [128 * D, 16], [1, 512]]), b2, writes=[b2])
                    return w1, b1, w2, b2

                nxt = wload(0)
                for db in range(D // 512):
                    w1, b1, w2, b2 = nxt
                    if db + 1 < D // 512:
                        nxt = wload(db + 1)
                    for tt in range(NT):
                        ot, ob = orr.next()
                        s.dma("sp", ot[:], bass.AP(self.oT, tt * TT, [[T, 128], [128 * T, 16], [1, TT]]), ob, writes=[ob])
                        for dc in range(4):
                            dchunk = db * 4 + dc
                            tl = []
                            for n in range(4):
                                pg, pgb = self.psb.next()
                                for k in range(2):
                                    s.op("pe", lambda e: e.matmul(pg[:], lhsT=w1[:, k, n, dc * 128:(dc + 1) * 128], rhs=gz[:, k, tt * TT:(tt + 1) * TT],
                                                                  start=(k == 0), stop=(k == 1)),
                                         reads=[b1, gzb], writes=[pgb] if k == 0 else (), pwrites=() if k == 0 else [pgb], signal=(k == 1))
                                gt, gtb = gts.next()
                                s.op("act", lambda e: e.activation(out=gt[:], in_=pg[:], func=AF.Sigmoid,
                                                                   bias=bg[:, n * KC + dchunk:n * KC + dchunk + 1], scale=1.0),
                                     reads=[pgb, bgb], writes=[gtb])
                                po, pob = self.psb.next()
                                for k in range(4):
                                    s.op("pe", lambda e: e.matmul(po[:], lhsT=w2[:, n * 4 + k, dc * 128:(dc + 1) * 128], rhs=ot[:, n * 4 + k, :],
                                                                  start=(k == 0), stop=(k == 3)),
                                         reads=[b2, ob], writes=[pob] if k == 0 else (), pwrites=() if k == 0 else [pob], signal=(k == 3))
                                tm, tmb = tms.next()
                                s.op("dve", lambda e: e.tensor_tensor(out=tm[:], in0=po[:], in1=gt[:], op=ALU.mult), reads=[pob, gtb], writes=[tmb])
                                tl.append((tm, tmb))
                            ma, mab = m1.next()
                            s.op("pool", lambda e: e.tensor_tensor(out=ma[:], in0=tl[0][0][:], in1=tl[1][0][:], op=ALU.add),
                                 reads=[tl[0][1], tl[1][1]], writes=[mab])
                            mb_, mbb = m1.next()
                            s.op("pool", lambda e: e.tensor_tensor(out=mb_[:], in0=tl[2][0][:], in1=tl[3][0][:], op=ALU.add),
                                 reads=[tl[2][1], tl[3][1]], writes=[mbb])
                            sg, sgb = sr.next()
                            s.op("pool", lambda e: e.tensor_tensor(out=sg[:], in0=ma[:], in1=mb_[:], op=ALU.add), reads=[mab, mbb], writes=[sgb])
                            s.dma("sp", self.mrg[tt, :, dchunk, :], sg[:], sgb, reads=[sgb])
                s.barrier()

    def phase_lin_res(self, wh, woff, kchunks, src, pieces, ncols_blk, tag):
        nc, s = self.nc, self.s
        NT = self.NT
        nb = D // ncols_blk
        nck = ncols_blk // 128
        with ExitStack() as st:
            wr = Ring([self.sb(st, tag + "_w%d" % i, [128, kchunks, ncols_blk], BF16) for i in range(2)])
            npiece = len(pieces)
            pmax = max(pieces)
            ar = Ring([self.sb(st, tag + "_a%d" % i, [128, pmax, TT], BF16) for i in range(3 if npiece > 1 else 2)])
            xr = Ring([self.sb(st, tag + "_xr%d" % i, [128, nck, TT], F32) for i in range(2)])
            yr = Ring([self.sb(st, tag + "_y%d" % i, [128, TT], F32) for i in range(4)])
            pstart = [sum(pieces[:i]) for i in range(npiece)]

            def wload(b):
                wt, wb = wr.next()
                s.dma("pool", wt[:], bass.AP(wh, woff + b * ncols_blk, [[D, 128], [128 * D, kchunks], [1, ncols_blk]]), wb, writes=[wb])
                return wt, wb

            def aload(tt, pi):
                at, ab = ar.next()
                s.dma("sp", at[:, 0:pieces[pi], :], src[tt, :, pstart[pi]:pstart[pi] + pieces[pi], :], ab, writes=[ab])
                return at, ab

            seq = [(b, tt, pi) for b in range(nb) for tt in range(NT) for pi in range(npiece)]
            nxt_w = wload(0)
            nxt_a = aload(0, 0)
            ai = 0
            for b in range(nb):
                wt, wb = nxt_w
                if b + 1 < nb:
                    nxt_w = wload(b + 1)
                for tt in range(NT):
                    xt, xb = xr.next()
                    s.dma("sp", xt[:], self.xres[tt, :, b * nck:(b + 1) * nck, :], xb, writes=[xb])
                    banks = [self.psb.next() for _ in range(nck)]
                    for pi in range(npiece):
                        at, ab = nxt_a
                        ai += 1
                        if ai < len(seq):
                            nxt_a = aload(seq[ai][1], seq[ai][2])
                        for c in range(nck):
                            pt, pb = banks[c]
                            for k in range(pieces[pi]):
                                kk = pstart[pi] + k
                                first, lastk = (kk == 0), (kk == kchunks - 1)
                                s.op("pe", lambda e: e.matmul(pt[:], lhsT=wt[:, kk, c * 128:(c + 1) * 128], rhs=at[:, k, :], start=first, stop=lastk),
                                     reads=[wb, ab], writes=[pb] if first else (), pwrites=() if first else [pb],
                                     signal=(lastk or k == pieces[pi] - 1))
                    for c in range(nck):
                        pt, pb = banks[c]
                        yt, yb = yr.next()
                        s.op("dve", lambda e: e.scalar_tensor_tensor(out=yt[:], in0=xt[:, c, :], scalar=float(ALPHA), in1=pt[:],
                                                                     op0=ALU.mult, op1=ALU.add), reads=[xb, pb], writes=[yb])
                        s.dma("sp", self.ypre[tt, :, b * nck + c, :], yt[:], yb, reads=[yb])
            s.barrier()

    def phase_p3a(self, l):
        nc, s = self.nc, self.s
        NT = self.NT
        FB = 256
        nb = D_FF // FB
        with ExitStack() as st:
            wr = Ring([self.sb(st, "p3_w%d" % i, [128, KC, 2, FB], BF16) for i in range(2)])
            xr = Ring([self.sb(st, "p3_x%d" % i, [128, KC, TT], BF16) for i in range(2)])
            sgr = Ring([self.sb(st, "p3_g%d" % i, [128, TT], F32) for i in range(3)])
            sr = Ring([self.sb(st, "p3_s%d" % i, [128, TT], BF16) for i in range(4)])

            def wload(b):
                wt, wb = wr.next()
                s.dma("pool", wt[:, :, 0, :], bass.AP(self.w_fi, l * D * 2 * D_FF + b * FB, [[2 * D_FF, 128], [128 * 2 * D_FF, KC], [1, FB]]), wb, writes=[wb])
                s.dma_more("pool", wt[:, :, 1, :], bass.AP(self.w_fi, l * D * 2 * D_FF + D_FF + b * FB, [[2 * D_FF, 128], [128 * 2 * D_FF, KC], [1, FB]]), wb, wb)
                return wt, wb

            def xload(tt):
                xt, xb = xr.next()
                s.dma("sp", xt[:], self.xbf[tt], xb, writes=[xb])
                return xt, xb

            nxt = wload(0)
            for b in range(nb):
                wt, wb = nxt
                if b + 1 < nb:
                    nxt = wload(b + 1)
                xn = xload(0)
                for tt in range(NT):
                    xt, xb = xn
                    if tt + 1 < NT:
                        xn = xload(tt + 1)
                    for c in range(FB // 128):
                        pg, pgb = self.psb.next()
                        pu, pub = self.psb.next()
                        for j, (pt, pb) in enumerate(((pg, pgb), (pu, pub))):
                            for k in range(KC):
                                s.op("pe", lambda e: e.matmul(pt[:], lhsT=wt[:, k, j, c * 128:(c + 1) * 128], rhs=xt[:, k, :], start=(k == 0), stop=(k == KC - 1)),
                                     reads=[wb, xb], writes=[pb] if k == 0 else (), pwrites=() if k == 0 else [pb], signal=(k == KC - 1))
                        sg, sgb = sgr.next()
                        s.op("act", lambda e: e.activation(out=sg[:], in_=pg[:], func=AF.Silu), reads=[pgb], writes=[sgb])
                        so, sob = sr.next()
                        s.op("dve", lambda e: e.tensor_tensor(out=so[:], in0=pu[:], in1=sg[:], op=ALU.mult), reads=[pub, sgb], writes=[sob])
                        s.dma("sp", self.ffT[tt, :, b * (FB // 128) + c, :], so[:], sob, reads=[sob])
            s.barrier()

    def layer(self, l):
        stop = [d for d in self.debug if d.startswith("stop_")]
        stop = stop[0] if stop else None
        self.phase_p1(l)
        self.phase_pa(l)
        self.phase_pb(l)
        self.phase_pc(l)
        self.phase_pd(l)
        self.phase_p2a(l)
        if stop == "stop_p2a":
            return True
        self.phase_lin_res(self.w_out, l * D * D, KC, self.mrg, [KC], 512, "p2b")
        if stop == "stop_p2b":
            return True
        self.phase_ln(self.ypre, l, self.ln1_g, self.ln1_b, None, last=False)
        if stop == "stop_ln1":
            return True
        self.phase_p3a(l)
        if stop == "stop_p3a":
            return True
        self.phase_lin_res(self.w_fo, l * D_FF * D, D_FF // 128, self.ffT, [22, 22, 21, 21], 256, "p3b")
        if stop == "stop_p3b":
            return True
        self.phase_ln(self.ypre, l, self.ln2_g, self.ln2_b, None, last=(l == self.L - 1))
        return False


def core_inputs(inp, x_tokens, seg, cross, n_layers):
    L = n_layers
    c = make_consts(seg, cross)
    rpb = np.zeros((L, 8, 31, 127), np.float32)
    rpb[:, :, 8:23, 48:79] = np.asarray(inp["na_rpb"], np.float32)[:L]
    f32 = lambda a: np.ascontiguousarray(np.asarray(a, np.float32))
    m = {
        "xin": tile_major(f32(x_tokens)),
        "ln_in_g": cols(inp["ln_in_g"]), "ln_in_b": cols(inp["ln_in_b"]),
        "w_in": f32(inp["w_in"][:L]),
        "na_rpb": rpb,
        "qk_norm_g": f32(inp["qk_norm_g"][:L]),
        "diff_lambda": f32(inp["diff_lambda"][:L]).reshape(L, 128),
        "diff_subln_g": f32(inp["diff_subln_g"][:L]),
        "t5_table": f32(inp["t5_table"]),
        "w_gate_down": f32(inp["w_gate_down"][:L]),
        "w_gate_up": f32(inp["w_gate_up"][:L]),
        "b_gate": np.stack([np.concatenate([cols(inp["b_gate"][l][n]) for n in range(4)], axis=1) for l in range(L)]),
        "w_branch": f32(inp["w_branch"][:L]).reshape(L, 2048, D),
        "w_out": f32(inp["w_out"][:L]),
        "ln1_g": np.stack([cols(inp["ln1_g"][l]) for l in range(L)]),
        "ln1_b": np.stack([cols(inp["ln1_b"][l]) for l in range(L)]),
        "w_ffn_in": f32(inp["w_ffn_in"][:L]),
        "w_ffn_out": f32(inp["w_ffn_out"][:L]),
        "ln2_g": np.stack([cols(inp["ln2_g"][l]) for l in range(L)]),
        "ln2_b": np.stack([cols(inp["ln2_b"][l]) for l in range(L)]),
    }
    m.update(c)
    return m


_PROG_CACHE = {}


def kernel(**inputs):
    seg = 2048
    L = DEPTH
    xp = np.asarray(inputs["x_prompt"], np.float32)
    xs = np.asarray(inputs["x_sample"], np.float32)
    if "prog" not in _PROG_CACHE:
        _PROG_CACHE["prog"] = Prog(L, seg).build()
    nc = _PROG_CACHE["prog"]
    shared = None
    in_maps = []
    for c in range(8):
        if c < 4:
            xt, cross = xp[c], True
        else:
            xt, cross = np.concatenate([xs[2 * (c - 4)], xs[2 * (c - 4) + 1]], axis=0), False
        if shared is None:
            m = core_inputs(inputs, xt, seg, cross, L)
            shared = m
        else:
            m = dict(shared)
            m["xin"] = tile_major(np.ascontiguousarray(xt, dtype=np.float32))
            m.update(make_consts(seg, cross))
        in_maps.append(m)
    res = run_bass_kernel_spmd(nc, in_maps, core_ids=list(range(8)))
    y_prompt = np.empty_like(xp)
    y_sample = np.empty_like(xs)
    for c in range(8):
        y = untile_major(np.asarray(res.results[c]["yout"]))
        if c < 4:
            y_prompt[c] = y
        else:
            y_sample[2 * (c - 4)] = y[:seg]
            y_sample[2 * (c - 4) + 1] = y[seg:]
    return (y_prompt, y_sample)
```
